# Optimizing a Trainium2 kernel written in Bass

```python
import jax
import jax.numpy as jnp
from jax import lax
import numpy as np

D_MODEL = 1024
BATCH = 4
SEQ = 8192
DEPTH = 2

GRID_W = 64
CTX_LEN = 256
EPS = 1e-6
N_MOD = 6
HEAD_DIM = 64
N_Q_HEADS = D_MODEL // 128
N_KV_HEADS = N_Q_HEADS // 4
Q_GROUP = N_Q_HEADS // N_KV_HEADS
Q_W = N_Q_HEADS * HEAD_DIM
KV_W = N_KV_HEADS * HEAD_DIM
Q_BLOCK = 128
ROPE_THETA = 10000.0
ROPE_AXIS_DIM = HEAD_DIM // 2
ROPE_FREQS = ROPE_AXIS_DIM // 2
GMLP_CHUNK = 128
GMLP_GROUPS = 4
GMLP_WIDTH = D_MODEL // 2
GMLP_GROUP_W = GMLP_WIDTH // GMLP_GROUPS
GLA_HEADS = 4
GLA_QK_W = D_MODEL // 4
GLA_V_W = D_MODEL // 2
GLA_DK = GLA_QK_W // GLA_HEADS
GLA_DV = GLA_V_W // GLA_HEADS
GLA_RANK = 16
GLA_TAU = 16.0
GLA_CHUNK = 64
FFN_HIDDEN = 128 * ((8 * D_MODEL // 3 + 127) // 128)
CONV_WIDTH = 3
IN_SPLITS = (GMLP_WIDTH, GMLP_WIDTH, Q_W, KV_W, KV_W, GLA_QK_W, GLA_QK_W, GLA_V_W, GLA_RANK, GLA_RANK, GLA_V_W, D_MODEL, D_MODEL, D_MODEL)
IN_WIDTH = sum(IN_SPLITS)

kernel_name = "hybrid_gated_branch_diffusion_block"


def rms_norm(x, g):
    xf = x.astype(jnp.float32)
    y = xf * lax.rsqrt(jnp.mean(xf * xf, axis=-1, keepdims=True) + EPS)
    return (y * g.astype(jnp.float32)).astype(x.dtype)


def adaln(cond, w, b):
    m = jax.nn.silu(cond) @ w + b
    return m.reshape(cond.shape[0], 1, N_MOD, D_MODEL)


def modulate(h, shift, scale):
    return h * (1 + scale) + shift


def split_in(p):
    out, start = [], 0
    for w in IN_SPLITS:
        out.append(p[..., start:start + w])
        start += w
    return out


def to_heads(p, n_heads, dim):
    return p.reshape(p.shape[0], p.shape[1], n_heads, dim)


def axial_rope_tables(n_tokens, dtype):
    rows = n_tokens // GRID_W
    t_row = jnp.repeat(jnp.arange(rows, dtype=jnp.int32), GRID_W)
    t_col = jnp.tile(jnp.arange(GRID_W, dtype=jnp.int32), rows)
    inv_freq = ROPE_THETA ** (-jnp.arange(ROPE_FREQS, dtype=jnp.float32) / ROPE_FREQS)
    ang_r = t_row.astype(jnp.float32)[:, None] * inv_freq
    ang_c = t_col.astype(jnp.float32)[:, None] * inv_freq
    return (jnp.cos(ang_r).astype(dtype)[:, None, :], jnp.sin(ang_r).astype(dtype)[:, None, :],
            jnp.cos(ang_c).astype(dtype)[:, None, :], jnp.sin(ang_c).astype(dtype)[:, None, :])


def rope_half(x, cos, sin):
    x1, x2 = x[..., :ROPE_FREQS], x[..., ROPE_FREQS:]
    return jnp.concatenate([x1 * cos - x2 * sin, x1 * sin + x2 * cos], axis=-1)


def apply_axial_rope(x, tables):
    cr, sr, cc, sc = tables
    return jnp.concatenate([rope_half(x[..., :ROPE_AXIS_DIM], cr, sr),
                            rope_half(x[..., ROPE_AXIS_DIM:], cc, sc)], axis=-1)


def block_attention(q, k, v):
    b, t = q.shape[0], q.shape[1]
    nb = t // Q_BLOCK
    qb = q.reshape(b, nb, Q_BLOCK, N_KV_HEADS, Q_GROUP, HEAD_DIM).transpose(1, 0, 2, 3, 4, 5)
    scale = HEAD_DIM ** -0.5

    def one_block(qi):
        s = jnp.einsum('bqhgd,bkhd->bhgqk', qi, k).astype(jnp.float32) * scale
        p = jax.nn.softmax(s, axis=-1).astype(v.dtype)
        return jnp.einsum('bhgqk,bkhd->bqhgd', p, v)

    o = lax.map(one_block, qb)
    return o.transpose(1, 0, 2, 3, 4, 5).reshape(b, t, Q_W)


def chunk_gmlp(u, v, norm_g, w_s, b_s):
    b, t, _ = u.shape
    n = t // GMLP_CHUNK
    u = jax.nn.gelu(u)
    v = rms_norm(jax.nn.gelu(v).reshape(b, n, GMLP_CHUNK, GMLP_GROUPS, GMLP_GROUP_W),
                 norm_g.reshape(GMLP_GROUPS, GMLP_GROUP_W))
    f = jnp.einsum('gij,bnjgc->bnigc', w_s, v) + b_s.T[:, :, None]
    return u * f.reshape(b, t, GMLP_WIDTH)


def gla_inputs(p_q, p_k, p_v, p_af, p_ab, w_a2, b_a):
    b, t, _ = p_q.shape
    q = to_heads(p_q, GLA_HEADS, GLA_DK) * (GLA_DK ** -0.5)
    k = to_heads(p_k, GLA_HEADS, GLA_DK)
    v = to_heads(p_v, GLA_HEADS, GLA_DV)

    def log_decay(a1, w2, b2):
        z = (a1 @ w2 + b2).astype(jnp.float32)
        return (jax.nn.log_sigmoid(z) / GLA_TAU).reshape(b, t, GLA_HEADS, GLA_DK)

    return q, k, v, log_decay(p_af, w_a2[0], b_a[0]), log_decay(p_ab, w_a2[1], b_a[1])


def gla_chunked(q, k, v, log_a, s0):
    b, t, h, dk = q.shape
    dv = v.shape[-1]
    n = t // GLA_CHUNK
    q = q.astype(jnp.float32).reshape(b, n, GLA_CHUNK, h, dk)
    k = k.astype(jnp.float32).reshape(b, n, GLA_CHUNK, h, dk)
    v = v.astype(jnp.float32).reshape(b, n, GLA_CHUNK, h, dv)
    cum = jnp.cumsum(log_a.astype(jnp.float32).reshape(b, n, GLA_CHUNK, h, dk), axis=2)
    cum_last = cum[:, :, -1:]
    q_in = q * jnp.exp(cum)
    k_in = k * jnp.exp(-cum)
    k_st = k * jnp.exp(cum_last - cum)
    mask = jnp.tril(jnp.ones((GLA_CHUNK, GLA_CHUNK), dtype=bool))
    att = jnp.where(mask, jnp.einsum('bnihd,bnjhd->bnhij', q_in, k_in), 0.0)
    o = jnp.einsum('bnhij,bnjhv->bnihv', att, v)
    u = jnp.einsum('bnjhd,bnjhv->nbhdv', k_st, v)
    decay = jnp.exp(cum_last[:, :, 0]).transpose(1, 0, 2, 3)

    def step(s, inp):
        d, u_n = inp
        return d[..., None] * s + u_n, s

    s_final, s_in = lax.scan(step, s0.astype(jnp.float32), (decay, u))
    o = o + jnp.einsum('bnihd,nbhdv->bnihv', q_in, s_in)
    return o.reshape(b, t, h, dv), s_final


def gla_chunked_reverse(q, k, v, log_a, s0):
    o, s = gla_chunked(q[:, ::-1], k[:, ::-1], v[:, ::-1], log_a[:, ::-1], s0)
    return o[:, ::-1], s


def gla_output(o, r, g):
    b, t = o.shape[0], o.shape[1]
    o = rms_norm(o, g.reshape(GLA_HEADS, GLA_DV)).reshape(b, t, GLA_V_W)
    return (o * jax.nn.silu(r.astype(jnp.float32))).astype(r.dtype)


def merge_branches(gate_logits, y_a, y_b, y_c, w_a, w_b, w_c, w_o):
    g_a, g_b, g_c = gate_logits
    merged = (jax.nn.sigmoid(g_a) * (y_a @ w_a) + jax.nn.sigmoid(g_b) * (y_b @ w_b)
              + jax.nn.sigmoid(g_c) * (y_c @ w_c))
    return merged @ w_o


def conv_ffn(h, w_up, cw, cb, w_down):
    t = h.shape[1]
    half = CONV_WIDTH // 2
    a = h @ w_up
    ap = jnp.pad(a, ((0, 0), (half, half), (0, 0)))
    a = cb + sum(ap[:, j:j + t] * cw[j] for j in range(CONV_WIDTH))
    g, val = jnp.split(a, 2, axis=-1)
    return (jax.nn.silu(g) * val) @ w_down


def setup_inputs(seed: int = 0) -> dict:
    key = jax.random.key(seed)
    ks = jax.random.split(key, 26)
    f32 = jnp.float32
    nrm = lambda k, shape, scale: jax.random.normal(k, shape, f32) * scale
    gain = lambda k, shape: 1.0 + 0.02 * jax.random.normal(k, shape, f32)
    F2 = 2 * FFN_HIDDEN
    return {
        'x': nrm(ks[0], (BATCH, SEQ, D_MODEL), 1.0),
        'c': nrm(ks[1], (BATCH, D_MODEL), 1.0),
        'ctx': nrm(ks[2], (BATCH, CTX_LEN, D_MODEL), 1.0),
        'c_ctx': nrm(ks[3], (D_MODEL,), 1.0),
        'w_ada': nrm(ks[4], (DEPTH, D_MODEL, N_MOD * D_MODEL), 0.5 * D_MODEL ** -0.5),
        'b_ada': nrm(ks[5], (DEPTH, N_MOD * D_MODEL), 0.02),
        'norm1_g': gain(ks[6], (DEPTH, D_MODEL)),
        'norm2_g': gain(ks[7], (DEPTH, D_MODEL)),
        'w_in': nrm(ks[8], (DEPTH, D_MODEL, IN_WIDTH), D_MODEL ** -0.5),
        'q_norm_g': gain(ks[9], (DEPTH, HEAD_DIM)),
        'k_norm_g': gain(ks[10], (DEPTH, HEAD_DIM)),
        'gmlp_norm_g': gain(ks[11], (DEPTH, GMLP_WIDTH)),
        'w_spatial': nrm(ks[12], (DEPTH, GMLP_GROUPS, GMLP_CHUNK, GMLP_CHUNK), 0.5 * GMLP_CHUNK ** -0.5),
        'b_spatial': gain(ks[13], (DEPTH, GMLP_GROUPS, GMLP_CHUNK)),
        'w_alpha2': nrm(ks[14], (DEPTH, 2, GLA_RANK, GLA_QK_W), GLA_RANK ** -0.5),
        'b_alpha': nrm(ks[15], (DEPTH, 2, GLA_QK_W), 0.02),
        'gla_norm_g': gain(ks[16], (DEPTH, GLA_V_W)),
        'w_br_a': nrm(ks[17], (DEPTH, GMLP_WIDTH, D_MODEL), GMLP_WIDTH ** -0.5),
        'w_br_b': nrm(ks[18], (DEPTH, Q_W, D_MODEL), Q_W ** -0.5),
        'w_br_c': nrm(ks[19], (DEPTH, GLA_V_W, D_MODEL), GLA_V_W ** -0.5),
        'w_out': nrm(ks[20], (DEPTH, D_MODEL, D_MODEL), D_MODEL ** -0.5),
        'w_ffn_up': nrm(ks[21], (DEPTH, D_MODEL, F2), D_MODEL ** -0.5),
        'conv_w': nrm(ks[22], (DEPTH, CONV_WIDTH, F2), CONV_WIDTH ** -0.5),
        'conv_b': nrm(ks[23], (DEPTH, F2), 0.02),
        'w_ffn_down': nrm(ks[24], (DEPTH, FFN_HIDDEN, D_MODEL), FFN_HIDDEN ** -0.5),
        'final_norm_g': gain(ks[25], (D_MODEL,)),
    }


def reference(x, c, ctx, c_ctx, w_ada, b_ada, norm1_g, norm2_g, w_in, q_norm_g, k_norm_g,
              gmlp_norm_g, w_spatial, b_spatial, w_alpha2, b_alpha, gla_norm_g, w_br_a, w_br_b,
              w_br_c, w_out, w_ffn_up, conv_w, conv_b, w_ffn_down, final_norm_g):
    rope_tab = axial_rope_tables(x.shape[1], x.dtype)
    xc = ctx
    for l in range(DEPTH):
        last = l == DEPTH - 1
        mod_x = adaln(c, w_ada[l], b_ada[l])
        mod_c = adaln(c_ctx[None], w_ada[l], b_ada[l])
        hx = modulate(rms_norm(x, norm1_g[l]), mod_x[:, :, 0], mod_x[:, :, 1])
        hc = modulate(rms_norm(xc, norm1_g[l]), mod_c[:, :, 0], mod_c[:, :, 1])
        px = split_in(hx @ w_in[l])
        pc = split_in(hc @ w_in[l])

        kc = rms_norm(to_heads(pc[3], N_KV_HEADS, HEAD_DIM), k_norm_g[l])
        vc = to_heads(pc[4], N_KV_HEADS, HEAD_DIM)
        qx = apply_axial_rope(rms_norm(to_heads(px[2], N_Q_HEADS, HEAD_DIM), q_norm_g[l]), rope_tab)
        kx = apply_axial_rope(rms_norm(to_heads(px[3], N_KV_HEADS, HEAD_DIM), k_norm_g[l]), rope_tab)
        vx = to_heads(px[4], N_KV_HEADS, HEAD_DIM)
        att_x = block_attention(qx, jnp.concatenate([kc, kx], axis=1), jnp.concatenate([vc, vx], axis=1))

        qgc, kgc, vgc, lfc, lbc = gla_inputs(*pc[5:10], w_alpha2[l], b_alpha[l])
        s0 = jnp.zeros((xc.shape[0], GLA_HEADS, GLA_DK, GLA_DV), jnp.float32)
        oc_f, sc_f = gla_chunked(qgc, kgc, vgc, lfc, s0)
        oc_b, sc_b = gla_chunked_reverse(qgc, kgc, vgc, lbc, s0)
        qgx, kgx, vgx, lfx, lbx = gla_inputs(*px[5:10], w_alpha2[l], b_alpha[l])
        ox_f, _ = gla_chunked(qgx, kgx, vgx, lfx, sc_f)
        ox_b, _ = gla_chunked_reverse(qgx, kgx, vgx, lbx, sc_b)
        gla_x = gla_output(ox_f + ox_b, px[10], gla_norm_g[l])

        gm_x = chunk_gmlp(px[0], px[1], gmlp_norm_g[l], w_spatial[l], b_spatial[l])

        mix_x = merge_branches(px[11:14], gm_x, att_x, gla_x, w_br_a[l], w_br_b[l], w_br_c[l], w_out[l])
        x_mid = x + mod_x[:, :, 2] * mix_x
        hx2 = modulate(rms_norm(x_mid, norm2_g[l]), mod_x[:, :, 3], mod_x[:, :, 4])
        x = x_mid + mod_x[:, :, 5] * conv_ffn(hx2, w_ffn_up[l], conv_w[l], conv_b[l], w_ffn_down[l])

        if not last:
            qc = rms_norm(to_heads(pc[2], N_Q_HEADS, HEAD_DIM), q_norm_g[l])
            att_c = block_attention(qc, kc, vc)
            gla_c = gla_output(oc_f + oc_b, pc[10], gla_norm_g[l])
            gm_c = chunk_gmlp(pc[0], pc[1], gmlp_norm_g[l], w_spatial[l], b_spatial[l])
            mix_c = merge_branches(pc[11:14], gm_c, att_c, gla_c, w_br_a[l], w_br_b[l], w_br_c[l], w_out[l])
            xc_mid = xc + mod_c[:, :, 2] * mix_c
            hc2 = modulate(rms_norm(xc_mid, norm2_g[l]), mod_c[:, :, 3], mod_c[:, :, 4])
            xc = xc_mid + mod_c[:, :, 5] * conv_ffn(hc2, w_ffn_up[l], conv_w[l], conv_b[l], w_ffn_down[l])
    return rms_norm(x, final_norm_g)
```

```python
import contextlib
import numpy as np
import concourse.bass as bass
import concourse.mybir as mybir
from concourse.bass_utils import run_bass_kernel_spmd

F32 = mybir.dt.float32
BF16 = mybir.dt.bfloat16
AF = mybir.ActivationFunctionType
ALU = mybir.AluOpType
AX = mybir.AxisListType

D = 1024
KC = 8
NCTX = 256
EPS = 1e-6
FH = 2816
NJ = 22
WCOL = {}
_o = 0
for _n, _w in (("ak", 128), ("av", 128), ("gk", 256), ("gv", 512), ("dec", 64), ("gq", 256), ("gr", 512),
               ("mu", 512), ("mv", 512), ("aq", 512), ("gA", 1024), ("gB", 1024), ("gC", 1024)):
    WCOL[_n] = (_o, _w)
    _o += _w
WIN_W = _o


class Trk:
    __slots__ = ("name", "w", "r")

    def __init__(self, name=""):
        self.name = name
        self.w = None
        self.r = {}


class T:
    __slots__ = ("ap", "trks")

    def __init__(self, ap, trks):
        self.ap = ap
        self.trks = trks if isinstance(trks, (list, tuple)) else [trks]

    def __getitem__(self, idx):
        return T(self.ap[idx], self.trks)

    def v(self, ap):
        return T(ap, self.trks)


class Prog:
    ENGS = ("pe", "act", "dve", "pool", "sp")

    def __init__(self, nc, n_dma_sems=24, epoch=30000, n_epochs=None):
        self.nc = nc
        self.eng = {"pe": nc.tensor, "act": nc.scalar, "dve": nc.vector, "pool": nc.gpsimd, "sp": nc.sync}
        self.epoch = epoch
        self.sems = {}
        self.cnt = {e: 0 for e in self.ENGS}
        self.waited = {e: {} for e in self.ENGS}
        self._cm = []
        n_epochs = n_epochs or {"pe": 8, "act": 4, "dve": 5, "pool": 1, "sp": 1}
        for e in self.ENGS:
            self._alloc((e, 0))
        self.dma_keys = []
        for i in range(n_dma_sems):
            self._alloc(("dma", i))
            self.dma_keys.append(("dma", i))
        self.dma_cnt = {k: 0 for k in self.dma_keys}
        self.dma_rr = 0
        self.n_wait = 0
        self._alloc(("sw", 0))
        self.dummy = None

    def _alloc(self, key):
        cm = self.nc.semaphore("s_" + "_".join(str(x) for x in key))
        self.sems[key] = cm.__enter__()
        self._cm.append(cm)

    def close(self):
        for cm in reversed(self._cm):
            cm.__exit__(None, None, None)

    def _wait(self, e, ev):
        if ev is None:
            return
        key, val = ev
        if e == "pe" and key[0] == "pe":
            return
        if self.waited[e].get(key, 0) >= val:
            return
        self.eng[e].wait_ge(self.sems[key], val)
        self.waited[e][key] = val
        self.n_wait += 1

    def _deps(self, e, reads, writes):
        for t in reads:
            for trk in t.trks:
                self._wait(e, trk.w)
        for t in writes:
            for trk in t.trks:
                self._wait(e, trk.w)
                for key, val in trk.r.items():
                    self._wait(e, (key, val))

    def _record(self, ev, reads, writes):
        for t in writes:
            for trk in t.trks:
                trk.w = ev
                trk.r = {}
        for t in reads:
            for trk in t.trks:
                if trk.r.get(ev[0], 0) < ev[1]:
                    trk.r[ev[0]] = ev[1]

    def I(self, e, fn, reads=(), writes=()):
        self._deps(e, reads, writes)
        inst = fn()
        self.cnt[e] += 1
        c = self.cnt[e]
        key = (e, (c - 1) // self.epoch)
        val = (c - 1) % self.epoch + 1
        if key not in self.sems:
            self._alloc(key)
        inst.then_inc(self.sems[key], 1)
        ev = (key, val)
        self._record(ev, reads, writes)
        return ev

    def dma(self, out, in_, q="sp"):
        reads, writes = [in_], [out]
        self._deps(q, reads, writes)
        if q == "pool":
            sem = self.sems[("sw", 0)]
            inst = self.eng[q].dma_start(out=out.ap, in_=in_.ap)
            inst.then_inc(sem, 16)
            self.eng[q].wait_ge(sem, 16)
            self.eng[q].sem_clear(sem)
            d = self.dummy
            ev = self.I("pool", lambda: self.nc.gpsimd.memset(d.ap, 0.0), reads=[], writes=[d])
            self._record(ev, reads, writes)
            return ev
        else:
            key = self.dma_keys[self.dma_rr]
            self.dma_rr = (self.dma_rr + 1) % len(self.dma_keys)
        if self.dma_cnt[key]:
            self._wait(q, (key, self.dma_cnt[key]))
        inst = self.eng[q].dma_start(out=out.ap, in_=in_.ap)
        self.dma_cnt[key] += 16
        inst.then_inc(self.sems[key], 16)
        ev = (key, self.dma_cnt[key])
        self._record(ev, reads, writes)
        return ev

    def barrier(self):
        for q in self.ENGS:
            self.finish(q)

    def finish(self, q="sp"):
        for key in list(self.dma_cnt):
            if self.dma_cnt[key]:
                self._wait(q, (key, self.dma_cnt[key]))
        for e in self.ENGS:
            c = self.cnt[e]
            if c and e != q:
                self._wait(q, ((e, (c - 1) // self.epoch), (c - 1) % self.epoch + 1))


def build(SEQ, DEPTH, dbg=False):
    import os
    STOP = os.environ.get("MK_STOP", "")
    SUB = float(os.environ.get("MK_SUB", "99"))
    stopped = [False]

    def phase_on(ph, l):
        if stopped[0]:
            return False
        if STOP and STOP == "%s%d" % (ph, l):
            stopped[0] = True
        return True
    nc = bass.Bass("TRN2", target_bir_lowering=False)
    NT = NCTX + SEQ
    NTILE = NT // 128
    blocks = [(0, NCTX, True)] + [(NCTX + 512 * i, 512, False) for i in range(SEQ // 512)]
    cblocks = [(0, 256, True)] + [(NCTX + 256 * i, 256, False) for i in range(SEQ // 256)]

    def din(name, shape, dt=F32):
        return nc.dram_tensor(name, list(shape), dt, kind="ExternalInput").ap()

    def dscr(name, shape, dt):
        if dbg:
            return nc.dram_tensor(name, list(shape), dt, kind="ExternalOutput").ap()
        return nc.dram_tensor(name, list(shape), dt).ap()

    xin = din("xin", [128, KC, NT])
    cT_d = din("cT", [128, KC, 2])
    w_ada_d = din("w_ada", [DEPTH, D, 6 * D])
    b_ada_d = din("b_ada", [DEPTH, 128, 48])
    n1g_d = din("n1g", [DEPTH, 128, KC])
    n2g_d = din("n2g", [DEPTH, 128, KC])
    fng_d = din("fng", [128, KC])
    w_in_d = din("w_in", [DEPTH, D, WIN_W])
    qg_d = din("qg", [DEPTH, 128, 1])
    kg_d = din("kg", [DEPTH, 128, 1])
    mng_d = din("mng", [DEPTH, 128, 512])
    wsT_d = din("wsT", [DEPTH, 128, 4, 128])
    bs_d = din("bs", [DEPTH, 1, 512])
    w2_d = din("w2", [DEPTH, 48, 256])
    ba_d = din("ba", [DEPTH, 128, 4])
    gng_d = din("gng", [DEPTH, 128, 512])
    wbr_d = [din("wbr%d" % i, [DEPTH, 512, D]) for i in range(3)]
    wout_d = din("wout", [DEPTH, D, D])
    wup_d = din("wup", [DEPTH, D, 2 * FH])
    cw_d = din("cw", [DEPTH, 128, 2 * NJ, 3])
    cb_d = din("cb", [DEPTH, 128, 2 * NJ])
    wdn_d = din("wdn", [DEPTH, FH, D])
    ropeC_d = din("ropeC", [128, SEQ])
    ropeS_d = din("ropeS", [128, SEQ])
    cst_d = din("cst", [128, 5, 512])
    outT = nc.dram_tensor("outT", [128, KC, SEQ], F32, kind="ExternalOutput").ap()

    hT_scr = dscr("hT_scr", [128, KC, NT], BF16)
    x1_scr = dscr("x1_scr", [128, KC, NT], F32)
    xm_scr = dscr("xm_scr", [128, KC, NT], F32)
    yc_scr = dscr("yc_scr", [128, 4, NT], BF16)
    gm_scr = dscr("gm_scr", [128, 4, NT], BF16)
    at_scr = dscr("at_scr", [64, 8, NT], BF16)
    ub_scr = dscr("ub_scr", [NTILE, 128, 256], F32)
    sin_scr = dscr("sin_scr", [2, NTILE, 128, 256], BF16)

    P = Prog(nc)
    ES = contextlib.ExitStack()

    _uid = [0]

    def sb(stack, name, shape, dt=F32):
        _uid[0] += 1
        t = stack.enter_context(nc.sbuf_tensor("sb%d_%s" % (_uid[0], name), list(shape), dt))
        return T(t[:], Trk(name))

    def mk_trk(n):
        return [Trk() for _ in range(n)]
    NB = len(blocks)
    trk = {nm: mk_trk(NT // 128) for nm in ("hT", "x1", "xm", "yc", "gm", "at", "ub", "sinf", "sinb", "xin")}
    const_trk = Trk("const")

    def dr(ap, nm, c0, n):
        t0, t1 = max(c0, 0) // 128, (min(c0 + n, NT) + 127) // 128
        return T(ap, trk[nm][t0:t1])

    def cdr(ap):
        return T(ap, const_trk)

    def rd(*xs):
        return [x for x in xs if isinstance(x, T)]

    def apv(x):
        return x.ap if isinstance(x, T) else x

    def mm(out, lhsT, rhs, start=True, stop=True):
        P.I("pe", lambda: nc.tensor.matmul(out.ap, lhsT.ap, rhs.ap, start=start, stop=stop),
            reads=[lhsT, rhs], writes=[out])

    def tr(out, in_, ident):
        P.I("pe", lambda: nc.tensor.transpose(out.ap, in_.ap, ident.ap), reads=[in_, ident], writes=[out])

    def act(out, in_, func, bias=None, scale=None):
        kw = {}
        if bias is not None:
            kw["bias"] = apv(bias)
        if scale is not None:
            kw["scale"] = apv(scale)
        P.I("act", lambda: nc.scalar.activation(out.ap, in_.ap, func, **kw),
            reads=[in_] + rd(bias, scale), writes=[out])

    def tt(out, a, b, op, e="dve"):
        eng = nc.vector if e == "dve" else nc.gpsimd
        P.I(e, lambda: eng.tensor_tensor(out.ap, a.ap, b.ap, op), reads=[a, b], writes=[out])

    def ts(out, a, s1, s2, op0, op1=None, e="dve"):
        eng = nc.vector if e == "dve" else nc.gpsimd
        if op1 is None:
            P.I(e, lambda: eng.tensor_scalar(out.ap, a.ap, apv(s1), None, op0), reads=[a] + rd(s1), writes=[out])
        else:
            P.I(e, lambda: eng.tensor_scalar(out.ap, a.ap, apv(s1), apv(s2), op0, op1),
                reads=[a] + rd(s1, s2), writes=[out])

    def stt(out, a, s, b, op0, op1):
        P.I("dve", lambda: nc.vector.scalar_tensor_tensor(out.ap, a.ap, apv(s), b.ap, op0, op1),
            reads=[a, b] + rd(s), writes=[out])

    def recip(out, a):
        P.I("dve", lambda: nc.vector.reciprocal(out.ap, a.ap), reads=[a], writes=[out])

    def red(out, a):
        P.I("dve", lambda: nc.vector.tensor_reduce(out.ap, a.ap, AX.X, ALU.add), reads=[a], writes=[out])

    def cp(out, a, e="dve"):
        if e == "act":
            P.I("act", lambda: nc.scalar.copy(out.ap, a.ap), reads=[a], writes=[out])
        else:
            eng = nc.vector if e == "dve" else nc.gpsimd
            P.I(e, lambda: eng.tensor_copy(out.ap, a.ap), reads=[a], writes=[out])

    def mset(out, val, e="pool"):
        eng = nc.vector if e == "dve" else nc.gpsimd
        P.I(e, lambda: eng.memset(out.ap, val), writes=[out])

    P.dummy = sb(ES, "dummy", [128, 1])
    wstage = None
    cst = sb(ES, "cst", [128, 5, 512])
    P.dma(cst, cdr(cst_d))
    ones_f = cst[:, 0, 0:128]
    blk1_f = cst[:, 0, 128:256]
    perm_f = cst[:, 0, 256:384]
    ident_f = cst[:, 0, 384:512]
    scanmask = cst[:, 1, :]
    maskF = cst[:, 2, :]
    maskB = cst[:, 3, :]
    ident_b = sb(ES, "ident_b", [128, 128], BF16)
    cp(ident_b, ident_f)
    ones_b = sb(ES, "ones_b", [128, 128], BF16)
    cp(ones_b, ones_f)
    cT = sb(ES, "cT", [128, KC, 2])
    P.dma(cT, cdr(cT_d))
    scT = sb(ES, "scT", [128, KC, 2])
    act(scT, cT, AF.Silu)
    fng = sb(ES, "fng", [128, KC])
    P.dma(fng, cdr(fng_d))
    zero8 = sb(ES, "zero8", [128, KC])
    mset(zero8, 0.0)

    _wst = [sb(ES, "wst%d" % i, [128, 512]) for i in range(2)]
    PS = [T(ES.enter_context(nc.psum_tensor("ps%d" % i, [128, 512], F32))[:], Trk("ps%d" % i)) for i in range(7)]
    PSB = T(ES.enter_context(nc.psum_tensor("psb", [128, 1024], BF16))[:], Trk("psb"))

    class Rot:
        def __init__(self, items):
            self.items, self.i = items, 0

        def __call__(self):
            x = self.items[self.i % len(self.items)]
            self.i += 1
            return x

    wstage = Rot(_wst)

    def wview(ap2d, kc, c0, w):
        return ap2d.rearrange("(kc p) n -> p kc n", p=128)[:, :, c0:c0 + w]

    def loadw(dst, src_ap):
        shp = list(dst.ap.shape)
        if len(shp) == 2:
            pieces = [(dst, src_ap)]
        else:
            pieces = [(dst[:, k, :], src_ap[:, k, :]) for k in range(shp[1])]
        for d_, s_ in pieces:
            p_, w_ = d_.ap.shape
            for c0 in range(0, w_, 512):
                w1 = min(512, w_ - c0)
                stg = wstage()
                P.dma(stg[0:p_, 0:w1], cdr(s_[:, c0:c0 + w1]))
                cp(d_[:, c0:c0 + w1], stg[0:p_, 0:w1], e="pool")

    def norm_mod(stk_tiles, xt, n, gs, shift, out):
        sq, ssum, rstd, tmpn, psn = stk_tiles
        act(sq[:, :, 0:n], xt, AF.Square)
        red(ssum[:, 0:n], sq[:, :, 0:n].v(sq.ap[:, :, 0:n].rearrange("p c n -> p n c")))
        mm(psn[:, 0:n], ones_f, ssum[:, 0:n])
        act(rstd[:, 0:n], psn[:, 0:n], AF.Sqrt, bias=EPS, scale=1.0 / D)
        recip(rstd[:, 0:n], rstd[:, 0:n])
        for c in range(KC):
            t = tmpn()
            stt(t[:, 0:n], xt[:, c, :], gs[:, c:c + 1], rstd[:, 0:n], ALU.mult, ALU.mult)
            act(out[:, c, :], t[:, 0:n], AF.Identity, bias=shift[:, c:c + 1], scale=1.0)

    for l in range(DEPTH):
        last = l == DEPTH - 1
        xsrc, xsrc_nm = (xin, "xin") if l == 0 else (x1_scr, "x1")
        LS = contextlib.ExitStack()
        mod = sb(LS, "mod", [128, 48, 2])
        bada = sb(LS, "bada", [128, 48])
        P.dma(bada, cdr(b_ada_d[l]))
        with contextlib.ExitStack() as st:
            wa = [sb(st, "wa%d" % i, [128, KC, 512]) for i in range(2)]
            for pc in range(12 if phase_on("P0", l) else 0):
                w = wa[pc % 2]
                P.dma(w, cdr(wview(w_ada_d[l], KC, pc * 512, 512)))
                ps = PS[pc % 2]
                for nb in range(4):
                    for kc in range(KC):
                        mm(ps[:, nb * 2:nb * 2 + 2], w[:, kc, nb * 128:(nb + 1) * 128], scT[:, kc, :],
                           start=kc == 0, stop=kc == KC - 1)
                for nb in range(4):
                    j = pc * 4 + nb
                    act(mod[:, j, :], ps[:, nb * 2:nb * 2 + 2], AF.Identity, bias=bada[:, j:j + 1], scale=1.0)
        P.barrier()
        n1g = sb(LS, "n1g", [128, KC])
        n2g = sb(LS, "n2g", [128, KC])
        P.dma(n1g, cdr(n1g_d[l]))
        P.dma(n2g, cdr(n2g_d[l]))
        gs1 = sb(LS, "gs1", [128, 2, KC])
        gs2 = sb(LS, "gs2", [128, 2, KC])
        for s in range(2):
            stt(gs1[:, s, :], mod[:, 8:16, s], 1.0, n1g, ALU.add, ALU.mult)
            stt(gs2[:, s, :], mod[:, 32:40, s], 1.0, n2g, ALU.add, ALU.mult)
        shift1 = lambda s: mod[:, 0:8, s]
        gate1 = lambda s: mod[:, 16:24, s]
        shift2 = lambda s: mod[:, 24:32, s]
        gate2 = lambda s: mod[:, 40:48, s]

        qg = sb(LS, "qg", [128, 1]); P.dma(qg, cdr(qg_d[l]))
        kg = sb(LS, "kg", [128, 1]); P.dma(kg, cdr(kg_d[l]))
        w2 = sb(LS, "w2", [48, 256]); P.dma(w2, cdr(w2_d[l]))
        nba = sb(LS, "nba", [128, 4]); P.dma(nba, cdr(ba_d[l]))
        ts(nba, nba, -1.0, None, ALU.mult)
        wdec = sb(LS, "wdec", [128, KC, 64], BF16)
        loadw(wdec, wview(w_in_d[l], KC, WCOL["dec"][0], 64))

        def gla_decay(hT, n, cum, tmp4, psr):
            ps_a = psr()
            for kc in range(KC):
                mm(ps_a[0:48, 0:n], wdec[:, kc, 0:48], hT[:, kc, 0:n], start=kc == 0, stop=kc == KC - 1)
            a_sb, e_sb, la_sb, ci_sb = tmp4
            cp(a_sb[0:48, 0:n], ps_a[0:48, 0:n], e="act")
            for d in range(2):
                for pr in range(2):
                    ps_z = psr()
                    mm(ps_z[:, 0:n], w2[d * 32:d * 32 + 16, pr * 128:(pr + 1) * 128], a_sb[d * 32:d * 32 + 16, 0:n])
                    act(e_sb[:, 0:n], ps_z[:, 0:n], AF.Exp, bias=nba[:, d * 2 + pr:d * 2 + pr + 1], scale=-1.0)
                    act(e_sb[:, 0:n], e_sb[:, 0:n], AF.Ln, bias=1.0, scale=1.0)
                    ts(la_sb[:, 0:n], e_sb[:, 0:n], -1.0 / 16.0, None, ALU.mult)
                    if d == 0:
                        P.I("dve", lambda: nc.vector.tensor_tensor_scan(
                            cum[0].ap[:, pr, 0:n], scanmask.ap[:, 0:n], la_sb.ap[:, 0:n], 0.0, ALU.mult, ALU.add),
                            reads=[scanmask, la_sb], writes=[cum[0]])
                    else:
                        P.I("dve", lambda: nc.vector.tensor_tensor_scan(
                            ci_sb.ap[:, 0:n], scanmask.ap[:, 0:n], la_sb.ap[:, 0:n], 0.0, ALU.mult, ALU.add),
                            reads=[scanmask, la_sb], writes=[ci_sb])
                        tt(la_sb[:, 0:n], la_sb[:, 0:n], ci_sb[:, 0:n], ALU.subtract)
                        for i in range(n // 128):
                            ts(cum[1][:, pr, i * 128:(i + 1) * 128], la_sb[:, i * 128:(i + 1) * 128],
                               ci_sb[:, i * 128 + 127:i * 128 + 128], None, ALU.add)

        def qk_norm(ps_in, n, gcol, rope_c0, out_bf, tmps, psr):
            sq, kn, t1 = tmps
            act(sq[:, 0:n], ps_in[:, 0:n], AF.Square)
            ps_s = psr()
            mm(ps_s[:, 0:n], blk1_f, sq[:, 0:n])
            act(sq[:, 0:n], ps_s[:, 0:n], AF.Sqrt, bias=EPS, scale=1.0 / 64)
            recip(sq[:, 0:n], sq[:, 0:n])
            if rope_c0 is None:
                stt(out_bf, ps_in[:, 0:n], gcol, sq[:, 0:n], ALU.mult, ALU.mult)
                return
            stt(kn[:, 0:n], ps_in[:, 0:n], gcol, sq[:, 0:n], ALU.mult, ALU.mult)
            ps_p = psr()
            mm(ps_p[:, 0:n], perm_f, kn[:, 0:n])
            rc, rs = tmps_rope
            tt(t1[:, 0:n], ps_p[:, 0:n], rs[:, 0:n], ALU.mult)
            tt(kn[:, 0:n], kn[:, 0:n], rc[:, 0:n], ALU.mult)
            tt(out_bf, kn[:, 0:n], t1[:, 0:n], ALU.add)

        MS = contextlib.ExitStack()
        KT = sb(MS, "KT", [128, NT], BF16)
        KT_trk = [Trk() for _ in blocks]
        VA = sb(MS, "VA", [128, NTILE, 2, 65], BF16)
        VA_trk = [Trk() for _ in blocks]
        mset(VA[:, :, :, 64:65], 1.0)
        for t_ in VA_trk:
            t_.w = VA.trks[0].w

        P.barrier()
        with contextlib.ExitStack() as st:
            wA = sb(st, "wA", [128, KC, 1024], BF16)
            loadw(wA, wview(w_in_d[l], KC, 0, 1024))
            xt = [sb(st, "xt%d" % i, [128, KC, 512]) for i in range(2)]
            sq = sb(st, "sq", [128, KC, 512])
            ssum = sb(st, "ssum", [128, 512])
            rstd = sb(st, "rstd", [128, 512])
            tmpn = Rot([sb(st, "tmpn%d" % i, [128, 512]) for i in range(2)])
            hTs = [sb(st, "hT%d" % i, [128, KC, 512], BF16) for i in range(2)]
            qsq = sb(st, "qsq", [128, 512]); qkn = sb(st, "qkn", [128, 512]); qt1 = sb(st, "qt1", [128, 512])
            rc = sb(st, "rc", [128, 512]); rs = sb(st, "rs", [128, 512])
            tmps_rope = (rc, rs)
            cum = [sb(st, "cum%d" % i, [128, 2, 512]) for i in range(2)]
            tmp4 = [sb(st, "g4_%d" % i, [128, 512]) for i in range(4)]
            Ee = Rot([sb(st, "Ee%d" % i, [128, 128]) for i in range(2)])
            kstT = Rot([sb(st, "kstT%d" % i, [128, 128], BF16) for i in range(2)])
            kst = sb(st, "kst", [128, 2, 2, 128], BF16)
            gk_sb = sb(st, "gk_sb", [128, 2, 512])
            gv = Rot([sb(st, "gv%d" % i, [128, 512], BF16) for i in range(2)])
            S = sb(st, "S", [128, 2, 128])
            mset(S, 0.0)
            sinst = Rot([sb(st, "sinst%d" % i, [128, 256], BF16) for i in range(2)])
            ubst = Rot([sb(st, "ubst%d" % i, [128, 256]) for i in range(2)])
            decf = sb(st, "decf", [128, 2, 4])
            decb = sb(st, "decb", [128, 2, NTILE])
            psr = Rot(PS[0:6])
            psn = PS[6]
            for bi, (c0, n, isctx) in enumerate(blocks if phase_on("PA", l) else []):
                s = 1 if isctx else 0
                x_t = xt[bi % 2]
                hT = hTs[bi % 2]
                P.dma(x_t[:, :, 0:n], dr(xsrc[:, :, c0:c0 + n], xsrc_nm, c0, n))
                norm_mod((sq, ssum, rstd, tmpn, psn), x_t[:, :, 0:n], n, gs1[:, s, :], shift1(s), hT[:, :, 0:n])
                P.dma(dr(hT_scr[:, :, c0:c0 + n], "hT", c0, n), hT[:, :, 0:n])
                ps_k = psr()
                for kc in range(KC):
                    mm(ps_k[:, 0:n], wA[:, kc, 0:128], hT[:, kc, 0:n], start=kc == 0, stop=kc == KC - 1)
                if not isctx:
                    P.dma(rc[:, 0:n], cdr(ropeC_d[:, c0 - NCTX:c0 - NCTX + n]))
                    P.dma(rs[:, 0:n], cdr(ropeS_d[:, c0 - NCTX:c0 - NCTX + n]))
                qk_norm(ps_k, n, kg, None if isctx else c0, T(KT.ap[:, c0:c0 + n], KT_trk[bi]), (qsq, qkn, qt1), psr)
                gla_decay(hT, n, cum, tmp4, psr)
                for pr in range(2):
                    ps_gk = psr()
                    for kc in range(KC):
                        mm(ps_gk[:, 0:n], wA[:, kc, 256 + pr * 128:256 + (pr + 1) * 128], hT[:, kc, 0:n],
                           start=kc == 0, stop=kc == KC - 1)
                    cp(gk_sb[:, pr, 0:n], ps_gk[:, 0:n], e="act")
                for pr in range(2):
                    act(decf[:, pr, 0:n // 128], cum[0][:, pr, 127:n:128], AF.Exp)
                    act(decb[:, pr, c0 // 128:(c0 + n) // 128], cum[1][:, pr, 0:n:128], AF.Exp)
                for i in range(n // 128):
                    gt = c0 // 128 + i
                    cs = slice(i * 128, (i + 1) * 128)
                    ps_v = psr()
                    for kc in range(KC):
                        mm(ps_v[:, 0:128], hT[:, kc, cs], wA[:, kc, 128:256], start=kc == 0, stop=kc == KC - 1)
                    cp(T(VA.ap[:, gt, :, 0:64], VA_trk[bi]),
                       ps_v[:, 0:128].v(ps_v.ap[:, 0:128].rearrange("p (h d) -> p h d", h=2)), e="act")
                    ps_gv = psr()
                    for kc in range(KC):
                        mm(ps_gv[:, :], hT[:, kc, cs], wA[:, kc, 512:1024], start=kc == 0, stop=kc == KC - 1)
                    g_v = gv()
                    cp(g_v, ps_gv, e="act")
                    for d in range(2):
                        for pr in range(2):
                            e_ = Ee()
                            last_col = (i * 128 + 127) if d == 0 else (i * 128)
                            act(e_, cum[d][:, pr, cs], AF.Exp, bias=cum[d][:, pr, last_col:last_col + 1], scale=-1.0)
                            k_ = kstT()
                            tt(k_, gk_sb[:, pr, cs], e_, ALU.mult)
                            tr(PSB[:, (d * 2 + pr) * 128:(d * 2 + pr + 1) * 128], k_, ident_b)
                    cp(kst.v(kst.ap.rearrange("p a b c -> p (a b c)")), PSB[:, 0:512], e="act")
                    ps_u = [psr(), psr()]
                    for d in range(2):
                        for h in range(4):
                            pr, hp = h // 2, h % 2
                            mm(ps_u[d][hp * 64:(hp + 1) * 64, pr * 128:(pr + 1) * 128],
                               kst[:, d, pr, hp * 64:(hp + 1) * 64], g_v[:, h * 128:(h + 1) * 128])
                    s_st = sinst()
                    cp(s_st, S.v(S.ap.rearrange("p a b -> p (a b)")), e="act")
                    P.dma(T(sin_scr[0, gt], trk["sinf"][gt]), s_st)
                    for pr in range(2):
                        stt(S[:, pr, :], S[:, pr, :], decf[:, pr, i:i + 1], ps_u[0][:, pr * 128:(pr + 1) * 128],
                            ALU.mult, ALU.add)
                    u_st = ubst()
                    cp(u_st, ps_u[1][:, 0:256], e="act")
                    P.dma(T(ub_scr[gt], trk["ub"][gt]), u_st)
            mset(S, 0.0)
            order = [1, 0] + list(range(NTILE - 1, 1, -1))
            for gt in (order if phase_on("PA2", l) else []):
                u_st = ubst()
                P.dma(u_st, T(ub_scr[gt], trk["ub"][gt]))
                s_st = sinst()
                cp(s_st, S.v(S.ap.rearrange("p a b -> p (a b)")), e="act")
                P.dma(T(sin_scr[1, gt], trk["sinb"][gt]), s_st)
                for pr in range(2):
                    stt(S[:, pr, :], S[:, pr, :], decb[:, pr, gt:gt + 1], u_st[:, pr * 128:(pr + 1) * 128],
                        ALU.mult, ALU.add)

        P.barrier()
        with contextlib.ExitStack() as st:
            wG = sb(st, "wG", [128, KC, 512], BF16)
            loadw(wG[:, :, 0:256], wview(w_in_d[l], KC, WCOL["gq"][0], 256))
            loadw(wG[:, :, 256:512], wview(w_in_d[l], KC, WCOL["gk"][0], 256))
            wV = sb(st, "wV", [128, KC, 512], BF16)
            loadw(wV, wview(w_in_d[l], KC, WCOL["gv"][0], 512))
            wR = sb(st, "wR", [128, KC, 512], BF16)
            loadw(wR, wview(w_in_d[l], KC, WCOL["gr"][0], 512))
            gng = sb(st, "gng", [128, 512]); P.dma(gng, cdr(gng_d[l]))
            hTs = [sb(st, "hTb%d" % i, [128, KC, 512], BF16) for i in range(2)]
            cum = [sb(st, "cumb%d" % i, [128, 2, 512]) for i in range(2)]
            tmp4 = [sb(st, "g4b_%d" % i, [128, 512]) for i in range(4)]
            E1 = sb(st, "E1", [128, 512])
            qin = [sb(st, "qin%d" % i, [128, 2, 512], BF16) for i in range(2)]
            kin = [sb(st, "kin%d" % i, [128, 2, 512], BF16) for i in range(2)]
            gv = Rot([sb(st, "gvb%d" % i, [128, 512], BF16) for i in range(2)])
            sr = Rot([sb(st, "sr%d" % i, [128, 512]) for i in range(2)])
            attm = [sb(st, "attm%d" % i, [128, 512], BF16) for i in range(2)]
            sinl = [Rot([sb(st, "sinl%d_%d" % (d, i), [128, 2, 128], BF16) for i in range(2)]) for d in range(2)]
            osq = sb(st, "osq", [128, 512]); oss = sb(st, "oss", [128, 4])
            on = sb(st, "on", [128, 512])
            yc = sb(st, "yc", [128, 512], BF16)
            ycT = Rot([sb(st, "ycT%d" % i, [128, 4, 512], BF16) for i in range(2)])
            psr = Rot(PS[0:7])
            for bi, (c0, n, isctx) in enumerate(blocks if phase_on("PB1", l) else []):
                if last and isctx:
                    continue
                hT = hTs[bi % 2]
                P.dma(hT[:, :, 0:n], dr(hT_scr[:, :, c0:c0 + n], "hT", c0, n))
                gla_decay(hT, n, cum, tmp4, psr)
                ps_q = [psr(), psr()]
                ps_kk = [psr(), psr()]
                for pr in range(2):
                    for kc in range(KC):
                        mm(ps_q[pr][:, 0:n], wG[:, kc, pr * 128:(pr + 1) * 128], hT[:, kc, 0:n],
                           start=kc == 0, stop=kc == KC - 1)
                    for kc in range(KC):
                        mm(ps_kk[pr][:, 0:n], wG[:, kc, 256 + pr * 128:256 + (pr + 1) * 128], hT[:, kc, 0:n],
                           start=kc == 0, stop=kc == KC - 1)
                for d in range(2):
                    for pr in range(2):
                        act(E1[:, 0:n], cum[d][:, pr, 0:n], AF.Exp)
                        stt(qin[d][:, pr, 0:n], ps_q[pr][:, 0:n], 0.125, E1[:, 0:n], ALU.mult, ALU.mult)
                        act(E1[:, 0:n], cum[d][:, pr, 0:n], AF.Exp, scale=-1.0)
                        tt(kin[d][:, pr, 0:n], ps_kk[pr][:, 0:n], E1[:, 0:n], ALU.mult)
                yT = ycT()
                for i in range(n // 128 if SUB >= 2 else 0):
                    gt = c0 // 128 + i
                    cs = slice(i * 128, (i + 1) * 128)
                    ps_gv = psr()
                    for kc in range(KC):
                        mm(ps_gv, hT[:, kc, cs], wV[:, kc, :], start=kc == 0, stop=kc == KC - 1)
                    g_v = gv()
                    cp(g_v, ps_gv, e="act")
                    ps_r = psr()
                    for kc in range(KC):
                        mm(ps_r, hT[:, kc, cs], wR[:, kc, :], start=kc == 0, stop=kc == KC - 1)
                    s_r = sr()
                    act(s_r, ps_r, AF.Silu)
                    if SUB < 2.5:
                        continue
                    sl = []
                    for d in range(2):
                        s_ = sinl[d]()
                        P.dma(s_.v(s_.ap.rearrange("p a b -> p (a b)")),
                              T(sin_scr[d, gt], trk["sinf" if d == 0 else "sinb"][gt]))
                        sl.append(s_)
                    for d in range(2):
                        pa = [psr(), psr()]
                        for h in range(4):
                            pr, hp = h // 2, h % 2
                            mm(pa[hp][:, pr * 128:(pr + 1) * 128], kin[d][hp * 64:(hp + 1) * 64, pr, cs],
                               qin[d][hp * 64:(hp + 1) * 64, pr, cs])
                        mk = maskF if d == 0 else maskB
                        for hp in range(2):
                            tt(attm[d].v(attm[d].ap.rearrange("p (pr hp t) -> p pr hp t", pr=2, hp=2)[:, :, hp, :]),
                               pa[hp][:, 0:256].v(pa[hp].ap[:, 0:256].rearrange("p (pr t) -> p pr t", pr=2)),
                               mk[:, 0:256].v(mk.ap[:, 0:256].rearrange("p (pr t) -> p pr t", pr=2)), ALU.mult)
                    po = [psr(), psr()]
                    for h in range(4):
                        pr, hp = h // 2, h % 2
                        oc = slice(h * 128, (h + 1) * 128)
                        od = po[hp][:, pr * 128:(pr + 1) * 128]
                        mm(od, attm[0][:, oc], g_v[:, oc], start=True, stop=False)
                        mm(od, attm[1][:, oc], g_v[:, oc], start=False, stop=False)
                        mm(od, qin[0][hp * 64:(hp + 1) * 64, pr, cs], sl[0][hp * 64:(hp + 1) * 64, pr, :],
                           start=False, stop=False)
                        mm(od, qin[1][hp * 64:(hp + 1) * 64, pr, cs], sl[1][hp * 64:(hp + 1) * 64, pr, :],
                           start=False, stop=True)
                    for hp in range(2):
                        act(osq.v(osq.ap.rearrange("p (pr hp t) -> p pr hp t", pr=2, hp=2)[:, :, hp, :]),
                            po[hp][:, 0:256].v(po[hp].ap[:, 0:256].rearrange("p (pr t) -> p pr t", pr=2)), AF.Square)
                    red(oss, osq.v(osq.ap.rearrange("p (h v) -> p h v", h=4)))
                    act(oss, oss, AF.Sqrt, bias=EPS, scale=1.0 / 128)
                    recip(oss, oss)
                    for h in range(4):
                        pr, hp = h // 2, h % 2
                        oc = slice(h * 128, (h + 1) * 128)
                        stt(on[:, oc], po[hp][:, pr * 128:(pr + 1) * 128], oss[:, h:h + 1], gng[:, oc], ALU.mult, ALU.mult)
                    tt(yc, on, s_r, ALU.mult)
                    if SUB < 7:
                        continue
                    for c in range(4):
                        tr(PSB[:, c * 128:(c + 1) * 128], yc[:, c * 128:(c + 1) * 128], ident_b)
                    cp(yT[:, :, cs], PSB[:, 0:512].v(PSB.ap[:, 0:512].rearrange("p (c t) -> p c t", c=4)), e="act")
                P.dma(dr(yc_scr[:, :, c0:c0 + n], "yc", c0, n), yT[:, :, 0:n])

        P.barrier()
        with contextlib.ExitStack() as st:
            wU = sb(st, "wU", [128, KC, 512], BF16)
            loadw(wU, wview(w_in_d[l], KC, WCOL["mu"][0], 512))
            wMV = sb(st, "wMV", [128, KC, 512], BF16)
            loadw(wMV, wview(w_in_d[l], KC, WCOL["mv"][0], 512))
            wsT = sb(st, "wsT", [128, 4, 128], BF16)
            loadw(wsT, wsT_d[l])
            bsr = sb(st, "bsr", [1, 512], BF16)
            loadw(bsr, bs_d[l])
            mng = sb(st, "mng", [128, 512]); P.dma(mng, cdr(mng_d[l]))
            hTs = [sb(st, "hTc%d" % i, [128, KC, 512], BF16) for i in range(2)]
            gu = sb(st, "gu", [128, 4, 512], BF16)
            gvv = sb(st, "gvv", [128, 512]); gsq = sb(st, "gsq", [128, 512]); gss = sb(st, "gss", [128, 4])
            gx2 = sb(st, "gx2", [128, 512]); gsg = sb(st, "gsg", [128, 512])
            vn = sb(st, "vn", [128, 512], BF16)
            gmT = Rot([sb(st, "gmT%d" % i, [128, 4, 512], BF16) for i in range(2)])
            psr = Rot(PS[0:7])

            def gelu(out, ps_in, n):
                act(gx2[:, 0:n], ps_in, AF.Square)
                ts(gx2[:, 0:n], gx2[:, 0:n], 0.044715, 1.0, ALU.mult, ALU.add)
                tt(gx2[:, 0:n], gx2[:, 0:n], ps_in, ALU.mult)
                act(gsg[:, 0:n], gx2[:, 0:n], AF.Sigmoid, scale=1.5957691216057308)
                tt(out, gsg[:, 0:n], ps_in, ALU.mult)

            for bi, (c0, n, isctx) in enumerate(blocks if phase_on("PB2", l) else []):
                if last and isctx:
                    continue
                hT = hTs[bi % 2]
                P.dma(hT[:, :, 0:n], dr(hT_scr[:, :, c0:c0 + n], "hT", c0, n))
                for g in range(4):
                    ps_u = psr()
                    for kc in range(KC):
                        mm(ps_u[:, 0:n], wU[:, kc, g * 128:(g + 1) * 128], hT[:, kc, 0:n], start=kc == 0, stop=kc == KC - 1)
                    gelu(gu[:, g, 0:n], ps_u[:, 0:n], n)
                gm = gmT()
                for i in range(n // 128):
                    cs = slice(i * 128, (i + 1) * 128)
                    ps_v = psr()
                    for kc in range(KC):
                        mm(ps_v, hT[:, kc, cs], wMV[:, kc, :], start=kc == 0, stop=kc == KC - 1)
                    gelu(gvv, ps_v, 512)
                    tt(gsq, gvv, gvv, ALU.mult)
                    red(gss, gsq.v(gsq.ap.rearrange("p (g c) -> p g c", g=4)))
                    act(gss, gss, AF.Sqrt, bias=EPS, scale=1.0 / 128)
                    recip(gss, gss)
                    for g in range(4):
                        oc = slice(g * 128, (g + 1) * 128)
                        stt(vn[:, oc], gvv[:, oc], gss[:, g:g + 1], mng[:, oc], ALU.mult, ALU.mult)
                    ps_f = psr()
                    for g in range(4):
                        oc = slice(g * 128, (g + 1) * 128)
                        mm(ps_f[:, oc], vn[:, oc], wsT[:, g, :], start=True, stop=False)
                        mm(ps_f[:, oc], ones_b[0:1, 0:128], bsr[0:1, oc], start=False, stop=True)
                    tt(gm[:, :, cs], gu[:, :, cs], ps_f.v(ps_f.ap.rearrange("p (g t) -> p g t", g=4)), ALU.mult)
                P.dma(dr(gm_scr[:, :, c0:c0 + n], "gm", c0, n), gm[:, :, 0:n])

        P.barrier()
        with contextlib.ExitStack() as st:
            wQ = sb(st, "wQ", [128, KC, 512], BF16)
            loadw(wQ, wview(w_in_d[l], KC, WCOL["aq"][0], 512))
            hTs = [sb(st, "hTd%d" % i, [128, KC, 512], BF16) for i in range(2)]
            qT = sb(st, "qT", [128, 4, 512], BF16)
            qsq = sb(st, "qsq3", [128, 512]); qkn = sb(st, "qkn3", [128, 512]); qt1 = sb(st, "qt13", [128, 512])
            rc = sb(st, "rc3", [128, 512]); rs = sb(st, "rs3", [128, 512])
            tmps_rope = (rc, rs)
            pT = Rot([sb(st, "pT%d" % i, [128, 512], BF16) for i in range(3)])
            rdn = Rot([sb(st, "rdn%d" % i, [128, 512]) for i in range(2)])
            o_sb = Rot([sb(st, "o_sb%d" % i, [64, 512]) for i in range(2)])
            atT = Rot([sb(st, "atT%d" % i, [64, 8, 512], BF16) for i in range(2)])
            nm8 = sb(st, "nm8", [128, 1]); mset(nm8, -8.0)
            accs = Rot(PS[0:2])
            pss = Rot(PS[2:4])
            psq = Rot(PS[4:7])
            for bi, (c0, n, isctx) in enumerate(blocks if phase_on("PB3", l) else []):
                if last and isctx:
                    continue
                hT = hTs[bi % 2]
                P.dma(hT[:, :, 0:n], dr(hT_scr[:, :, c0:c0 + n], "hT", c0, n))
                if not isctx:
                    P.dma(rc[:, 0:n], cdr(ropeC_d[:, c0 - NCTX:c0 - NCTX + n]))
                    P.dma(rs[:, 0:n], cdr(ropeS_d[:, c0 - NCTX:c0 - NCTX + n]))
                for c in range(4):
                    ps_q = psq()
                    for kc in range(KC):
                        mm(ps_q[:, 0:n], wQ[:, kc, c * 128:(c + 1) * 128], hT[:, kc, 0:n], start=kc == 0, stop=kc == KC - 1)
                    qk_norm(ps_q, n, qg, None if isctx else c0, qT[:, c, 0:n], (qsq, qkn, qt1), psq)
                nkt = 2 if isctx else NTILE
                aT = atT()
                for h in range(8):
                    c, hp = h % 4, h // 4
                    prt = slice(hp * 64, (hp + 1) * 64)
                    acc = accs()
                    for kt in range(nkt):
                        kb = 0 if kt < 2 else 1 + (kt - 2) // 4
                        ps_s = pss()
                        mm(ps_s[:, 0:n], T(KT.ap[prt, kt * 128:(kt + 1) * 128], KT_trk[kb]), qT[prt, c, 0:n])
                        p_ = pT()
                        act(p_[:, 0:n], ps_s[:, 0:n], AF.Exp, bias=nm8, scale=0.125)
                        mm(acc[0:65, 0:n], T(VA.ap[:, kt, hp, :], VA_trk[kb]), p_[:, 0:n],
                           start=kt == 0, stop=kt == nkt - 1)
                    r_ = rdn()
                    recip(r_[64:65, 0:n], acc[64:65, 0:n])
                    ps_b = psq()
                    mm(ps_b[:, 0:n], ones_f[64:65, 0:128], r_[64:65, 0:n])
                    o_ = o_sb()
                    cp(o_[:, 0:n], acc[0:64, 0:n], e="act")
                    tt(aT[:, h, 0:n], o_[:, 0:n], ps_b[0:64, 0:n], ALU.mult)
                P.dma(dr(at_scr[:, :, c0:c0 + n], "at", c0, n), aT[:, :, 0:n])
        P.barrier()
        MS.close()

        P.barrier()
        with contextlib.ExitStack() as st:
            wgt = [sb(st, "wgt%d" % i, [128, KC, 1024], BF16) for i in range(3)]
            for i, nm in enumerate(("gA", "gB", "gC")):
                for hh in range(2):
                    loadw(wgt[i][:, :, hh * 512:(hh + 1) * 512], wview(w_in_d[l], KC, WCOL[nm][0] + hh * 512, 512))
            wbr = [sb(st, "wbr%d" % i, [128, 4, 1024] if i != 1 else [64, 8, 1024], BF16) for i in range(3)]
            for i in range(3):
                if i == 1:
                    loadw(wbr[i], wbr_d[i][l].rearrange("(h p) n -> p h n", p=64))
                else:
                    loadw(wbr[i], wview(wbr_d[i][l], 4, 0, 1024))
            wo = sb(st, "wo", [128, KC, 1024], BF16)
            for hh in range(2):
                loadw(wo[:, :, hh * 512:(hh + 1) * 512], wview(wout_d[l], KC, hh * 512, 512))
            hTs = [sb(st, "hTe%d" % i, [128, KC, 512], BF16) for i in range(2)]
            srcs = [[sb(st, "src%d_%d" % (k, i), [128, 4, 512] if k != 1 else [64, 8, 512], BF16) for i in range(1)]
                    for k in range(3)]
            xts = [sb(st, "xte%d" % i, [128, KC, 512]) for i in range(1)]
            sg = Rot([sb(st, "sg%d" % i, [128, 512]) for i in range(3)])
            tb = [sb(st, "tb%d" % i, [128, 512]) for i in range(3)]
            mrg = sb(st, "mrg", [128, KC, 512], BF16)
            xmo = Rot([sb(st, "xmo%d" % i, [128, KC, 512]) for i in range(1)])
            psr = Rot(PS[0:7])
            scr_list = ((gm_scr, "gm"), (at_scr, "at"), (yc_scr, "yc"))
            for bi, (c0, n, isctx) in enumerate(blocks if phase_on("PB4", l) else []):
                if last and isctx:
                    continue
                s = 1 if isctx else 0
                hT = hTs[bi % 2]
                P.dma(hT[:, :, 0:n], dr(hT_scr[:, :, c0:c0 + n], "hT", c0, n))
                sr_ = []
                for k in range(3):
                    t_ = srcs[k][0]
                    P.dma(t_[:, :, 0:n], dr(scr_list[k][0][:, :, c0:c0 + n], scr_list[k][1], c0, n))
                    sr_.append(t_)
                x_t = xts[0]
                P.dma(x_t[:, :, 0:n], dr(xsrc[:, :, c0:c0 + n], xsrc_nm, c0, n))
                for m in range(KC):
                    mc = slice(m * 128, (m + 1) * 128)
                    for k in range(3):
                        ps_g = psr()
                        for kc in range(KC):
                            mm(ps_g[:, 0:n], wgt[k][:, kc, mc], hT[:, kc, 0:n], start=kc == 0, stop=kc == KC - 1)
                        s_g = sg()
                        act(s_g[:, 0:n], ps_g[:, 0:n], AF.Sigmoid)
                        ps_y = psr()
                        nk = 8 if k == 1 else 4
                        for kc in range(nk):
                            mm(ps_y[:, 0:n], wbr[k][:, kc, mc], sr_[k][:, kc, 0:n], start=kc == 0, stop=kc == nk - 1)
                        tt(tb[k][:, 0:n], ps_y[:, 0:n], s_g[:, 0:n], ALU.mult)
                    tt(tb[0][:, 0:n], tb[0][:, 0:n], tb[1][:, 0:n], ALU.add)
                    tt(mrg[:, m, 0:n], tb[0][:, 0:n], tb[2][:, 0:n], ALU.add)
                xo = xmo()
                for m in range(KC):
                    mc = slice(m * 128, (m + 1) * 128)
                    ps_o = psr()
                    for kc in range(KC):
                        mm(ps_o[:, 0:n], wo[:, kc, mc], mrg[:, kc, 0:n], start=kc == 0, stop=kc == KC - 1)
                    stt(xo[:, m, 0:n], ps_o[:, 0:n], gate1(s)[:, m:m + 1], x_t[:, m, 0:n], ALU.mult, ALU.add)
                P.dma(dr(xm_scr[:, :, c0:c0 + n], "xm", c0, n), xo[:, :, 0:n])

        P.barrier()
        with contextlib.ExitStack() as st:
            wup = sb(st, "wup", [128, KC, 2 * FH], BF16)
            for pc in range(11):
                loadw(wup[:, :, pc * 512:(pc + 1) * 512], wview(wup_d[l], KC, pc * 512, 512))
            wdn = sb(st, "wdn", [128, NJ, 1024], BF16)
            for hh in range(4):
                loadw(wdn[:, :, hh * 256:(hh + 1) * 256], wview(wdn_d[l], NJ, hh * 256, 256))
            cw = sb(st, "cw", [128, 2 * NJ, 3]); P.dma(cw, cdr(cw_d[l]))
            cb = sb(st, "cb", [128, 2 * NJ]); P.dma(cb, cdr(cb_d[l]))
            NW = 258
            xms = [sb(st, "xms%d" % i, [128, KC, NW]) for i in range(1)]
            sq = sb(st, "sqc", [128, KC, NW])
            ssum = sb(st, "ssumc", [128, NW])
            rstd = sb(st, "rstdc", [128, NW])
            tmpn = Rot([sb(st, "tmpnc%d" % i, [128, NW]) for i in range(2)])
            h2 = sb(st, "h2", [128, KC, NW], BF16)
            a_g = sb(st, "a_g", [128, NW]); a_v = sb(st, "a_v", [128, NW])
            cvg = sb(st, "cvg", [128, 256]); cvv = sb(st, "cvv", [128, 256])
            actT = sb(st, "actT", [128, NJ, 256], BF16)
            xo2 = Rot([sb(st, "xo2_%d" % i, [128, KC, 256]) for i in range(1)])
            fo = Rot([sb(st, "fo%d" % i, [128, KC, 256]) for i in range(1)])
            psr = Rot(PS[0:6])
            psn = PS[6]
            for bi, (c0, n, isctx) in enumerate(cblocks if phase_on("PC", l) else []):
                if last and isctx:
                    continue
                s = 1 if isctx else 0
                lo_end = c0 == 0 or c0 == NCTX
                hi_end = c0 + n == NCTX or c0 + n == NT
                xm = xms[0]
                a0 = c0 - 1 if not lo_end else c0
                a1 = c0 + n + 1 if not hi_end else c0 + n
                o0 = a0 - (c0 - 1)
                if lo_end:
                    mset(xm[:, :, 0:1], 0.0)
                if hi_end:
                    mset(xm[:, :, n + 1:n + 2], 0.0)
                P.dma(xm[:, :, o0:o0 + (a1 - a0)], dr(xm_scr[:, :, a0:a1], "xm", a0, a1 - a0))
                norm_mod((sq, ssum, rstd, tmpn, psn), xm, NW, gs2[:, s, :], shift2(s), h2)
                for j in range(NJ):
                    for half, a_sb, cv in ((0, a_g, cvg), (1, a_v, cvv)):
                        jj = half * NJ + j
                        ps = psr()
                        for kc in range(KC):
                            mm(ps[:, 0:NW], wup[:, kc, jj * 128:(jj + 1) * 128], h2[:, kc, :], start=kc == 0, stop=kc == KC - 1)
                        cp(a_sb, ps[:, 0:NW], e="act")
                        if lo_end:
                            mset(a_sb[:, 0:1], 0.0)
                        if hi_end:
                            mset(a_sb[:, n + 1:n + 2], 0.0)
                        ts(cv, a_sb[:, 0:n], cw[:, jj, 0:1], cb[:, jj:jj + 1], ALU.mult, ALU.add)
                        stt(cv, a_sb[:, 1:n + 1], cw[:, jj, 1:2], cv, ALU.mult, ALU.add)
                        stt(cv, a_sb[:, 2:n + 2], cw[:, jj, 2:3], cv, ALU.mult, ALU.add)
                    act(cvg, cvg, AF.Silu)
                    tt(actT[:, j, :], cvg, cvv, ALU.mult)
                xo = xo2()
                for m in range(KC):
                    mc = slice(m * 128, (m + 1) * 128)
                    ps_o = psr()
                    for j in range(NJ):
                        mm(ps_o[:, 0:n], wdn[:, j, mc], actT[:, j, :], start=j == 0, stop=j == NJ - 1)
                    stt(xo[:, m, :], ps_o[:, 0:n], gate2(s)[:, m:m + 1], xm[:, m, 1:n + 1], ALU.mult, ALU.add)
                if not last:
                    P.dma(dr(x1_scr[:, :, c0:c0 + n], "x1", c0, n), xo)
                else:
                    f_ = fo()
                    norm_mod((sq, ssum, rstd, tmpn, psn), xo, n, fng, zero8, f_)
                    P.dma(cdr(outT[:, :, c0 - NCTX:c0 - NCTX + n]), f_)
        P.barrier()
        LS.close()

    P.finish("sp")
    ES.close()
    P.close()
    return nc


def _fm(v):
    kc = v.shape[-1] // 128
    return np.ascontiguousarray(np.swapaxes(v.reshape(v.shape[:-1] + (kc, 128)), -1, -2))


def _consts(SEQ):
    cst = np.zeros((128, 5, 512), np.float32)
    cst[:, 0, 0:128] = 1.0
    for hb in range(2):
        cst[hb * 64:(hb + 1) * 64, 0, 128 + hb * 64:128 + (hb + 1) * 64] = 1.0
    for m in range(128):
        d = m % 64
        base = m - d
        blk, r = d // 32, d % 32
        if r < 16:
            cst[base + blk * 32 + r + 16, 0, 256 + m] = -1.0
        else:
            cst[base + blk * 32 + r - 16, 0, 256 + m] = 1.0
    cst[:, 0, 384:512] = np.eye(128, dtype=np.float32)
    cst[:, 1, :] = 1.0
    cst[:, 1, 0::128] = 0.0
    j = np.arange(128)[:, None]
    i = np.arange(128)[None, :]
    cst[:, 2, :] = np.tile((j <= i).astype(np.float32), (1, 4))
    cst[:, 3, :] = np.tile((j >= i).astype(np.float32), (1, 4))
    t = np.arange(SEQ)
    inv = (10000.0 ** (-np.arange(16, dtype=np.float32) / 16)).astype(np.float32)
    ang_r = (t // 64).astype(np.float32)[None, :] * inv[:, None]
    ang_c = (t % 64).astype(np.float32)[None, :] * inv[:, None]
    C = np.zeros((128, SEQ), np.float32)
    S = np.zeros((128, SEQ), np.float32)
    for m in range(128):
        d = m % 64
        ang = ang_r if d < 32 else ang_c
        C[m] = np.cos(ang[d % 16]).astype(np.float32)
        S[m] = np.sin(ang[d % 16]).astype(np.float32)
    return cst, C, S


def _prep(inp, SEQ, DEPTH):
    f = np.float32
    g = {k: np.asarray(v) for k, v in inp.items()}
    IN = {"mu": (0, 512), "mv": (512, 512), "aq": (1024, 512), "ak": (1536, 128), "av": (1664, 128),
          "gq": (1792, 256), "gk": (2048, 256), "gv": (2304, 512), "af": (2816, 16), "ab": (2832, 16),
          "gr": (2848, 512), "gA": (3360, 1024), "gB": (4384, 1024), "gC": (5408, 1024)}
    w_in = g["w_in"]
    win = np.zeros((DEPTH, D, WIN_W), f)
    for nm, (o, w) in WCOL.items():
        if nm == "dec":
            win[:, :, o:o + 16] = w_in[:, :, IN["af"][0]:IN["af"][0] + 16]
            win[:, :, o + 32:o + 48] = w_in[:, :, IN["ab"][0]:IN["ab"][0] + 16]
        elif nm == "aq":
            src = w_in[:, :, IN["aq"][0]:IN["aq"][0] + 512].reshape(DEPTH, D, 8, 64)
            order = [0, 4, 1, 5, 2, 6, 3, 7]
            win[:, :, o:o + 512] = src[:, :, order, :].reshape(DEPTH, D, 512)
        else:
            win[:, :, o:o + w] = w_in[:, :, IN[nm][0]:IN[nm][0] + w]
    cst, C, S = _consts(SEQ)
    shared = {
        "w_ada": g["w_ada"].astype(f), "b_ada": _fm(g["b_ada"].reshape(DEPTH, 48 * 128)).reshape(DEPTH, 128, 48),
        "n1g": _fm(g["norm1_g"]), "n2g": _fm(g["norm2_g"]), "fng": _fm(g["final_norm_g"]),
        "w_in": win,
        "qg": np.tile(g["q_norm_g"], (1, 2)).reshape(DEPTH, 128, 1).astype(f),
        "kg": np.tile(g["k_norm_g"], (1, 2)).reshape(DEPTH, 128, 1).astype(f),
        "mng": np.ascontiguousarray(np.broadcast_to(g["gmlp_norm_g"][:, None, :], (DEPTH, 128, 512))).astype(f),
        "wsT": np.ascontiguousarray(np.transpose(g["w_spatial"], (0, 3, 1, 2))).astype(f),
        "bs": g["b_spatial"].reshape(DEPTH, 1, 512).astype(f),
        "gng": np.ascontiguousarray(np.broadcast_to(g["gla_norm_g"][:, None, :], (DEPTH, 128, 512))).astype(f),
        "wbr0": g["w_br_a"], "wbr1": g["w_br_b"], "wbr2": g["w_br_c"],
        "wout": g["w_out"], "wup": g["w_ffn_up"], "wdn": g["w_ffn_down"],
        "ropeC": C, "ropeS": S, "cst": cst,
    }
    w2 = np.zeros((DEPTH, 48, 256), f)
    w2[:, 0:16] = g["w_alpha2"][:, 0]
    w2[:, 32:48] = g["w_alpha2"][:, 1]
    shared["w2"] = w2
    ba = g["b_alpha"].reshape(DEPTH, 2, 2, 128)
    shared["ba"] = np.ascontiguousarray(np.transpose(ba, (0, 3, 1, 2))).reshape(DEPTH, 128, 4).astype(f)
    cwv = g["conv_w"].reshape(DEPTH, 3, 2 * NJ, 128)
    shared["cw"] = np.ascontiguousarray(np.transpose(cwv, (0, 3, 2, 1))).astype(f)
    shared["cb"] = np.ascontiguousarray(np.transpose(g["conv_b"].reshape(DEPTH, 2 * NJ, 128), (0, 2, 1))).astype(f)
    shared = {k: np.ascontiguousarray(v, dtype=f) for k, v in shared.items()}
    shared["b_ada"] = np.ascontiguousarray(
        np.transpose(g["b_ada"].reshape(DEPTH, 48, 128), (0, 2, 1))).astype(f)
    B = g["x"].shape[0]
    per = []
    for b in range(B):
        X = np.concatenate([g["ctx"][b], g["x"][b]], 0)
        xin = np.ascontiguousarray(np.transpose(X.reshape(-1, KC, 128), (2, 1, 0))).astype(f)
        cc = np.stack([g["c"][b], g["c_ctx"]], -1)
        cT = np.ascontiguousarray(np.transpose(cc.reshape(KC, 128, 2), (1, 0, 2))).astype(f)
        per.append({"xin": xin, "cT": cT})
    return shared, per


_NC_CACHE = {}


def run(inp, dbg=False, n_cores=8):
    x = np.asarray(inp["x"])
    B, SEQ, _ = x.shape
    DEPTH = np.asarray(inp["w_in"]).shape[0]
    key = (SEQ, DEPTH, dbg)
    if key not in _NC_CACHE:
        _NC_CACHE[key] = build(SEQ, DEPTH, dbg)
    nc = _NC_CACHE[key]
    shared, per = _prep(inp, SEQ, DEPTH)
    in_maps = []
    for c in range(n_cores):
        m = dict(shared)
        m.update(per[c % B])
        in_maps.append(m)
    res = run_bass_kernel_spmd(nc, in_maps, core_ids=list(range(n_cores)))
    out = np.empty((B, SEQ, D), np.float32)
    for b in range(B):
        o = res.results[b]["outT"]
        out[b] = np.transpose(o, (2, 1, 0)).reshape(SEQ, D)
    return out, res


def kernel(**inputs):
    out, _ = run(inputs)
    return out
```

```python
import contextlib
import numpy as np
import concourse.bass as bass
import concourse.mybir as mybir
from concourse.bass_utils import run_bass_kernel_spmd

F32 = mybir.dt.float32
BF16 = mybir.dt.bfloat16
AF = mybir.ActivationFunctionType
ALU = mybir.AluOpType
AX = mybir.AxisListType

D = 1024
KC = 8
NCTX = 256
EPS = 1e-6
FH = 2816
NJ = 22
WCOL = {}
_o = 0
for _n, _w in (("ak", 128), ("av", 128), ("gk", 256), ("gv", 512), ("dec", 64), ("gq", 256), ("gr", 512),
               ("mu", 512), ("mv", 512), ("aq", 512), ("gA", 1024), ("gB", 1024), ("gC", 1024)):
    WCOL[_n] = (_o, _w)
    _o += _w
WIN_W = _o


class Trk:
    __slots__ = ("name", "w", "r")

    def __init__(self, name=""):
        self.name = name
        self.w = None
        self.r = {}


class T:
    __slots__ = ("ap", "trks")

    def __init__(self, ap, trks):
        self.ap = ap
        self.trks = trks if isinstance(trks, (list, tuple)) else [trks]

    def __getitem__(self, idx):
        return T(self.ap[idx], self.trks)

    def v(self, ap):
        return T(ap, self.trks)


class Prog:
    ENGS = ("pe", "act", "dve", "pool", "sp")

    def __init__(self, nc, n_dma_sems=24, epoch=30000, n_epochs=None):
        self.nc = nc
        self.eng = {"pe": nc.tensor, "act": nc.scalar, "dve": nc.vector, "pool": nc.gpsimd, "sp": nc.sync}
        self.epoch = epoch
        self.sems = {}
        self.cnt = {e: 0 for e in self.ENGS}
        self.waited = {e: {} for e in self.ENGS}
        self._cm = []
        n_epochs = n_epochs or {"pe": 8, "act": 4, "dve": 5, "pool": 1, "sp": 1}
        for e in self.ENGS:
            self._alloc((e, 0))
        self.dma_keys = []
        for i in range(n_dma_sems):
            self._alloc(("dma", i))
            self.dma_keys.append(("dma", i))
        self.dma_cnt = {k: 0 for k in self.dma_keys}
        self.dma_rr = 0
        self.n_wait = 0
        self._alloc(("sw", 0))
        self.dummy = None

    def _alloc(self, key):
        cm = self.nc.semaphore("s_" + "_".join(str(x) for x in key))
        self.sems[key] = cm.__enter__()
        self._cm.append(cm)

    def close(self):
        for cm in reversed(self._cm):
            cm.__exit__(None, None, None)

    def _wait(self, e, ev):
        if ev is None:
            return
        key, val = ev
        if e == "pe" and key[0] == "pe":
            return
        if self.waited[e].get(key, 0) >= val:
            return
        self.eng[e].wait_ge(self.sems[key], val)
        self.waited[e][key] = val
        self.n_wait += 1

    def _deps(self, e, reads, writes):
        for t in reads:
            for trk in t.trks:
                self._wait(e, trk.w)
        for t in writes:
            for trk in t.trks:
                self._wait(e, trk.w)
                for key, val in trk.r.items():
                    self._wait(e, (key, val))

    def _record(self, ev, reads, writes):
        for t in writes:
            for trk in t.trks:
                trk.w = ev
                trk.r = {}
        for t in reads:
            for trk in t.trks:
                if trk.r.get(ev[0], 0) < ev[1]:
                    trk.r[ev[0]] = ev[1]

    def I(self, e, fn, reads=(), writes=()):
        self._deps(e, reads, writes)
        inst = fn()
        self.cnt[e] += 1
        c = self.cnt[e]
        key = (e, (c - 1) // self.epoch)
        val = (c - 1) % self.epoch + 1
        if key not in self.sems:
            self._alloc(key)
        inst.then_inc(self.sems[key], 1)
        ev = (key, val)
        self._record(ev, reads, writes)
        return ev

    def dma(self, out, in_, q="sp"):
        reads, writes = [in_], [out]
        self._deps(q, reads, writes)
        if q == "pool":
            sem = self.sems[("sw", 0)]
            inst = self.eng[q].dma_start(out=out.ap, in_=in_.ap)
            inst.then_inc(sem, 16)
            self.eng[q].wait_ge(sem, 16)
            self.eng[q].sem_clear(sem)
            d = self.dummy
            ev = self.I("pool", lambda: self.nc.gpsimd.memset(d.ap, 0.0), reads=[], writes=[d])
            self._record(ev, reads, writes)
            return ev
        else:
            key = self.dma_keys[self.dma_rr]
            self.dma_rr = (self.dma_rr + 1) % len(self.dma_keys)
        if self.dma_cnt[key]:
            self._wait(q, (key, self.dma_cnt[key]))
        inst = self.eng[q].dma_start(out=out.ap, in_=in_.ap)
        self.dma_cnt[key] += 16
        inst.then_inc(self.sems[key], 16)
        ev = (key, self.dma_cnt[key])
        self._record(ev, reads, writes)
        return ev

    def barrier(self):
        for q in self.ENGS:
            self.finish(q)

    def finish(self, q="sp"):
        for key in list(self.dma_cnt):
            if self.dma_cnt[key]:
                self._wait(q, (key, self.dma_cnt[key]))
        for e in self.ENGS:
            c = self.cnt[e]
            if c and e != q:
                self._wait(q, ((e, (c - 1) // self.epoch), (c - 1) % self.epoch + 1))


def build(SEQ, DEPTH, dbg=False):
    import os
    STOP = os.environ.get("MK_STOP", "")
    SUB = float(os.environ.get("MK_SUB", "99"))
    stopped = [False]

    def phase_on(ph, l):
        if stopped[0]:
            return False
        if STOP and STOP == "%s%d" % (ph, l):
            stopped[0] = True
        return True
    nc = bass.Bass("TRN2", target_bir_lowering=False)
    NT = NCTX + SEQ
    NTILE = NT // 128
    blocks = [(0, NCTX, True)] + [(NCTX + 512 * i, 512, False) for i in range(SEQ // 512)]
    cblocks = [(0, 256, True)] + [(NCTX + 256 * i, 256, False) for i in range(SEQ // 256)]

    def din(name, shape, dt=F32):
        return nc.dram_tensor(name, list(shape), dt, kind="ExternalInput").ap()

    def dscr(name, shape, dt):
        if dbg:
            return nc.dram_tensor(name, list(shape), dt, kind="ExternalOutput").ap()
        return nc.dram_tensor(name, list(shape), dt).ap()

    xin = din("xin", [128, KC, NT])
    cT_d = din("cT", [128, KC, 2])
    w_ada_d = din("w_ada", [DEPTH, D, 6 * D])
    b_ada_d = din("b_ada", [DEPTH, 128, 48])
    n1g_d = din("n1g", [DEPTH, 128, KC])
    n2g_d = din("n2g", [DEPTH, 128, KC])
    fng_d = din("fng", [128, KC])
    w_in_d = din("w_in", [DEPTH, D, WIN_W])
    qg_d = din("qg", [DEPTH, 128, 1])
    kg_d = din("kg", [DEPTH, 128, 1])
    mng_d = din("mng", [DEPTH, 128, 512])
    wsT_d = din("wsT", [DEPTH, 128, 4, 128])
    bs_d = din("bs", [DEPTH, 1, 512])
    w2_d = din("w2", [DEPTH, 48, 256])
    ba_d = din("ba", [DEPTH, 128, 4])
    gng_d = din("gng", [DEPTH, 128, 512])
    wbr_d = [din("wbr%d" % i, [DEPTH, 512, D]) for i in range(3)]
    wout_d = din("wout", [DEPTH, D, D])
    wup_d = din("wup", [DEPTH, D, 2 * FH])
    cw_d = din("cw", [DEPTH, 128, 2 * NJ, 3])
    cb_d = din("cb", [DEPTH, 128, 2 * NJ])
    wdn_d = din("wdn", [DEPTH, FH, D])
    ropeC_d = din("ropeC", [128, SEQ])
    ropeS_d = din("ropeS", [128, SEQ])
    cst_d = din("cst", [128, 5, 512])
    outT = nc.dram_tensor("outT", [128, KC, SEQ], F32, kind="ExternalOutput").ap()

    hT_scr = dscr("hT_scr", [128, KC, NT], BF16)
    x1_scr = dscr("x1_scr", [128, KC, NT], F32)
    xm_scr = dscr("xm_scr", [128, KC, NT], F32)
    yc_scr = dscr("yc_scr", [128, 4, NT], BF16)
    gm_scr = dscr("gm_scr", [128, 4, NT], BF16)
    at_scr = dscr("at_scr", [128, 4, NT], BF16)
    ub_scr = dscr("ub_scr", [NTILE, 128, 256], F32)
    sin_scr = dscr("sin_scr", [2, NTILE, 128, 256], BF16)

    P = Prog(nc)
    ES = contextlib.ExitStack()

    _uid = [0]

    def sb(stack, name, shape, dt=F32):
        _uid[0] += 1
        t = stack.enter_context(nc.sbuf_tensor("sb%d_%s" % (_uid[0], name), list(shape), dt))
        return T(t[:], Trk(name))

    def mk_trk(n):
        return [Trk() for _ in range(n)]
    NB = len(blocks)
    trk = {nm: mk_trk(NT // 128) for nm in ("hT", "x1", "xm", "yc", "gm", "at", "ub", "sinf", "sinb", "xin")}
    const_trk = Trk("const")

    def dr(ap, nm, c0, n):
        t0, t1 = max(c0, 0) // 128, (min(c0 + n, NT) + 127) // 128
        return T(ap, trk[nm][t0:t1])

    def cdr(ap):
        return T(ap, const_trk)

    def rd(*xs):
        return [x for x in xs if isinstance(x, T)]

    def apv(x):
        return x.ap if isinstance(x, T) else x

    def mm(out, lhsT, rhs, start=True, stop=True):
        P.I("pe", lambda: nc.tensor.matmul(out.ap, lhsT.ap, rhs.ap, start=start, stop=stop),
            reads=[lhsT, rhs], writes=[out])

    def tr(out, in_, ident):
        P.I("pe", lambda: nc.tensor.transpose(out.ap, in_.ap, ident.ap), reads=[in_, ident], writes=[out])

    def act(out, in_, func, bias=None, scale=None):
        kw = {}
        if bias is not None:
            kw["bias"] = apv(bias)
        if scale is not None:
            kw["scale"] = apv(scale)
        P.I("act", lambda: nc.scalar.activation(out.ap, in_.ap, func, **kw),
            reads=[in_] + rd(bias, scale), writes=[out])

    def tt(out, a, b, op, e="dve"):
        eng = nc.vector if e == "dve" else nc.gpsimd
        P.I(e, lambda: eng.tensor_tensor(out.ap, a.ap, b.ap, op), reads=[a, b], writes=[out])

    def ts(out, a, s1, s2, op0, op1=None, e="dve"):
        eng = nc.vector if e == "dve" else nc.gpsimd
        if op1 is None:
            P.I(e, lambda: eng.tensor_scalar(out.ap, a.ap, apv(s1), None, op0), reads=[a] + rd(s1), writes=[out])
        else:
            P.I(e, lambda: eng.tensor_scalar(out.ap, a.ap, apv(s1), apv(s2), op0, op1),
                reads=[a] + rd(s1, s2), writes=[out])

    def stt(out, a, s, b, op0, op1):
        P.I("dve", lambda: nc.vector.scalar_tensor_tensor(out.ap, a.ap, apv(s), b.ap, op0, op1),
            reads=[a, b] + rd(s), writes=[out])

    def recip(out, a):
        P.I("dve", lambda: nc.vector.reciprocal(out.ap, a.ap), reads=[a], writes=[out])

    def red(out, a):
        P.I("dve", lambda: nc.vector.tensor_reduce(out.ap, a.ap, AX.X, ALU.add), reads=[a], writes=[out])

    def cp(out, a, e="dve"):
        if e == "act":
            P.I("act", lambda: nc.scalar.copy(out.ap, a.ap), reads=[a], writes=[out])
        else:
            eng = nc.vector if e == "dve" else nc.gpsimd
            P.I(e, lambda: eng.tensor_copy(out.ap, a.ap), reads=[a], writes=[out])

    def mset(out, val, e="pool"):
        eng = nc.vector if e == "dve" else nc.gpsimd
        P.I(e, lambda: eng.memset(out.ap, val), writes=[out])

    P.dummy = sb(ES, "dummy", [128, 1])
    wstage = None
    cst = sb(ES, "cst", [128, 5, 512])
    P.dma(cst, cdr(cst_d))
    ones_f = cst[:, 0, 0:128]
    blk1_f = cst[:, 0, 128:256]
    perm_f = cst[:, 0, 256:384]
    ident_f = cst[:, 0, 384:512]
    scanmask = cst[:, 1, :]
    maskF = cst[:, 2, :]
    maskB = cst[:, 3, :]
    ident_b = sb(ES, "ident_b", [128, 128], BF16)
    cp(ident_b, ident_f)
    ones_b = sb(ES, "ones_b", [128, 128], BF16)
    cp(ones_b, ones_f)
    cT = sb(ES, "cT", [128, KC, 2])
    P.dma(cT, cdr(cT_d))
    scT = sb(ES, "scT", [128, KC, 2])
    act(scT, cT, AF.Silu)
    fng = sb(ES, "fng", [128, KC])
    P.dma(fng, cdr(fng_d))
    zero8 = sb(ES, "zero8", [128, KC])
    mset(zero8, 0.0)

    _wst = [sb(ES, "wst%d" % i, [128, 512]) for i in range(2)]
    PS = [T(ES.enter_context(nc.psum_tensor("ps%d" % i, [128, 512], F32))[:], Trk("ps%d" % i)) for i in range(7)]
    PSB = T(ES.enter_context(nc.psum_tensor("psb", [128, 1024], BF16))[:], Trk("psb"))

    class Rot:
        def __init__(self, items):
            self.items, self.i = items, 0

        def __call__(self):
            x = self.items[self.i % len(self.items)]
            self.i += 1
            return x

    wstage = Rot(_wst)

    def wview(ap2d, kc, c0, w):
        return ap2d.rearrange("(kc p) n -> p kc n", p=128)[:, :, c0:c0 + w]

    def loadw(dst, src_ap):
        shp = list(dst.ap.shape)
        if len(shp) == 2:
            pieces = [(dst, src_ap)]
        else:
            pieces = [(dst[:, k, :], src_ap[:, k, :]) for k in range(shp[1])]
        for d_, s_ in pieces:
            p_, w_ = d_.ap.shape
            for c0 in range(0, w_, 512):
                w1 = min(512, w_ - c0)
                stg = wstage()
                P.dma(stg[0:p_, 0:w1], cdr(s_[:, c0:c0 + w1]))
                cp(d_[:, c0:c0 + w1], stg[0:p_, 0:w1], e="pool")

    def norm_mod(stk_tiles, xt, n, gs, shift, out):
        sq, ssum, rstd, tmpn, psn = stk_tiles
        act(sq[:, :, 0:n], xt, AF.Square)
        red(ssum[:, 0:n], sq[:, :, 0:n].v(sq.ap[:, :, 0:n].rearrange("p c n -> p n c")))
        mm(psn[:, 0:n], ones_f, ssum[:, 0:n])
        act(rstd[:, 0:n], psn[:, 0:n], AF.Sqrt, bias=EPS, scale=1.0 / D)
        recip(rstd[:, 0:n], rstd[:, 0:n])
        for c in range(KC):
            t = tmpn()
            stt(t[:, 0:n], xt[:, c, :], gs[:, c:c + 1], rstd[:, 0:n], ALU.mult, ALU.mult)
            act(out[:, c, :], t[:, 0:n], AF.Identity, bias=shift[:, c:c + 1], scale=1.0)

    for l in range(DEPTH):
        last = l == DEPTH - 1
        xsrc, xsrc_nm = (xin, "xin") if l == 0 else (x1_scr, "x1")
        LS = contextlib.ExitStack()
        mod = sb(LS, "mod", [128, 48, 2])
        bada = sb(LS, "bada", [128, 48])
        P.dma(bada, cdr(b_ada_d[l]))
        with contextlib.ExitStack() as st:
            wa = [sb(st, "wa%d" % i, [128, KC, 512]) for i in range(2)]
            for pc in range(12 if phase_on("P0", l) else 0):
                w = wa[pc % 2]
                P.dma(w, cdr(wview(w_ada_d[l], KC, pc * 512, 512)))
                ps = PS[pc % 2]
                for nb in range(4):
                    for kc in range(KC):
                        mm(ps[:, nb * 2:nb * 2 + 2], w[:, kc, nb * 128:(nb + 1) * 128], scT[:, kc, :],
                           start=kc == 0, stop=kc == KC - 1)
                for nb in range(4):
                    j = pc * 4 + nb
                    act(mod[:, j, :], ps[:, nb * 2:nb * 2 + 2], AF.Identity, bias=bada[:, j:j + 1], scale=1.0)
        P.barrier()
        n1g = sb(LS, "n1g", [128, KC])
        n2g = sb(LS, "n2g", [128, KC])
        P.dma(n1g, cdr(n1g_d[l]))
        P.dma(n2g, cdr(n2g_d[l]))
        gs1 = sb(LS, "gs1", [128, 2, KC])
        gs2 = sb(LS, "gs2", [128, 2, KC])
        for s in range(2):
            stt(gs1[:, s, :], mod[:, 8:16, s], 1.0, n1g, ALU.add, ALU.mult)
            stt(gs2[:, s, :], mod[:, 32:40, s], 1.0, n2g, ALU.add, ALU.mult)
        shift1 = lambda s: mod[:, 0:8, s]
        gate1 = lambda s: mod[:, 16:24, s]
        shift2 = lambda s: mod[:, 24:32, s]
        gate2 = lambda s: mod[:, 40:48, s]

        qg = sb(LS, "qg", [128, 1]); P.dma(qg, cdr(qg_d[l]))
        kg = sb(LS, "kg", [128, 1]); P.dma(kg, cdr(kg_d[l]))
        w2 = sb(LS, "w2", [48, 256]); P.dma(w2, cdr(w2_d[l]))
        nba = sb(LS, "nba", [128, 4]); P.dma(nba, cdr(ba_d[l]))
        ts(nba, nba, -1.0, None, ALU.mult)
        wdec = sb(LS, "wdec", [128, KC, 64], BF16)
        loadw(wdec, wview(w_in_d[l], KC, WCOL["dec"][0], 64))

        def gla_decay(hT, n, cum, tmp4, psr):
            ps_a = psr()
            for kc in range(KC):
                mm(ps_a[0:48, 0:n], wdec[:, kc, 0:48], hT[:, kc, 0:n], start=kc == 0, stop=kc == KC - 1)
            a_sb, e_sb, la_sb, ci_sb = tmp4
            cp(a_sb[0:48, 0:n], ps_a[0:48, 0:n], e="act")
            for d in range(2):
                for pr in range(2):
                    ps_z = psr()
                    mm(ps_z[:, 0:n], w2[d * 32:d * 32 + 16, pr * 128:(pr + 1) * 128], a_sb[d * 32:d * 32 + 16, 0:n])
                    act(e_sb[:, 0:n], ps_z[:, 0:n], AF.Exp, bias=nba[:, d * 2 + pr:d * 2 + pr + 1], scale=-1.0)
                    act(e_sb[:, 0:n], e_sb[:, 0:n], AF.Ln, bias=1.0, scale=1.0)
                    ts(la_sb[:, 0:n], e_sb[:, 0:n], -1.0 / 16.0, None, ALU.mult)
                    if d == 0:
                        P.I("dve", lambda: nc.vector.tensor_tensor_scan(
                            cum[0].ap[:, pr, 0:n], scanmask.ap[:, 0:n], la_sb.ap[:, 0:n], 0.0, ALU.mult, ALU.add),
                            reads=[scanmask, la_sb], writes=[cum[0]])
                    else:
                        P.I("dve", lambda: nc.vector.tensor_tensor_scan(
                            ci_sb.ap[:, 0:n], scanmask.ap[:, 0:n], la_sb.ap[:, 0:n], 0.0, ALU.mult, ALU.add),
                            reads=[scanmask, la_sb], writes=[ci_sb])
                        tt(la_sb[:, 0:n], la_sb[:, 0:n], ci_sb[:, 0:n], ALU.subtract)
                        for i in range(n // 128):
                            ts(cum[1][:, pr, i * 128:(i + 1) * 128], la_sb[:, i * 128:(i + 1) * 128],
                               ci_sb[:, i * 128 + 127:i * 128 + 128], None, ALU.add)

        def qk_norm(ps_in, n, gcol, rope_c0, out_bf, tmps, psr):
            sq, kn, t1 = tmps
            act(sq[:, 0:n], ps_in[:, 0:n], AF.Square)
            ps_s = psr()
            mm(ps_s[:, 0:n], blk1_f, sq[:, 0:n])
            act(sq[:, 0:n], ps_s[:, 0:n], AF.Sqrt, bias=EPS, scale=1.0 / 64)
            recip(sq[:, 0:n], sq[:, 0:n])
            if rope_c0 is None:
                stt(out_bf, ps_in[:, 0:n], gcol, sq[:, 0:n], ALU.mult, ALU.mult)
                return
            stt(kn[:, 0:n], ps_in[:, 0:n], gcol, sq[:, 0:n], ALU.mult, ALU.mult)
            ps_p = psr()
            mm(ps_p[:, 0:n], perm_f, kn[:, 0:n])
            rc, rs = tmps_rope
            tt(t1[:, 0:n], ps_p[:, 0:n], rs[:, 0:n], ALU.mult)
            tt(kn[:, 0:n], kn[:, 0:n], rc[:, 0:n], ALU.mult)
            tt(out_bf, kn[:, 0:n], t1[:, 0:n], ALU.add)

        MS = contextlib.ExitStack()
        KT = sb(MS, "KT", [128, NT], BF16)
        KT_trk = [Trk() for _ in blocks]
        VA = sb(MS, "VA", [128, NTILE, 2, 65], BF16)
        VA_trk = [Trk() for _ in blocks]
        mset(VA[:, :, :, 64:65], 1.0)
        for t_ in VA_trk:
            t_.w = VA.trks[0].w

        P.barrier()
        with contextlib.ExitStack() as st:
            wA = sb(st, "wA", [128, KC, 1024], BF16)
            loadw(wA, wview(w_in_d[l], KC, 0, 1024))
            xt = [sb(st, "xt%d" % i, [128, KC, 512]) for i in range(2)]
            sq = sb(st, "sq", [128, KC, 512])
            ssum = sb(st, "ssum", [128, 512])
            rstd = sb(st, "rstd", [128, 512])
            tmpn = Rot([sb(st, "tmpn%d" % i, [128, 512]) for i in range(2)])
            hTs = [sb(st, "hT%d" % i, [128, KC, 512], BF16) for i in range(2)]
            qsq = sb(st, "qsq", [128, 512]); qkn = sb(st, "qkn", [128, 512]); qt1 = sb(st, "qt1", [128, 512])
            rc = sb(st, "rc", [128, 512]); rs = sb(st, "rs", [128, 512])
            tmps_rope = (rc, rs)
            cum = [sb(st, "cum%d" % i, [128, 2, 512]) for i in range(2)]
            tmp4 = [sb(st, "g4_%d" % i, [128, 512]) for i in range(4)]
            Ee = Rot([sb(st, "Ee%d" % i, [128, 128]) for i in range(2)])
            kstT = Rot([sb(st, "kstT%d" % i, [128, 128], BF16) for i in range(2)])
            kst = sb(st, "kst", [128, 2, 2, 128], BF16)
            gk_sb = sb(st, "gk_sb", [128, 2, 512])
            gv = Rot([sb(st, "gv%d" % i, [128, 512], BF16) for i in range(2)])
            S = sb(st, "S", [128, 2, 128])
            mset(S, 0.0)
            sinst = Rot([sb(st, "sinst%d" % i, [128, 256], BF16) for i in range(2)])
            ubst = Rot([sb(st, "ubst%d" % i, [128, 256]) for i in range(2)])
            decf = sb(st, "decf", [128, 2, 4])
            decb = sb(st, "decb", [128, 2, NTILE])
            psr = Rot(PS[0:6])
            psn = PS[6]
            for bi, (c0, n, isctx) in enumerate(blocks if phase_on("PA", l) else []):
                s = 1 if isctx else 0
                x_t = xt[bi % 2]
                hT = hTs[bi % 2]
                P.dma(x_t[:, :, 0:n], dr(xsrc[:, :, c0:c0 + n], xsrc_nm, c0, n))
                norm_mod((sq, ssum, rstd, tmpn, psn), x_t[:, :, 0:n], n, gs1[:, s, :], shift1(s), hT[:, :, 0:n])
                P.dma(dr(hT_scr[:, :, c0:c0 + n], "hT", c0, n), hT[:, :, 0:n])
                ps_k = psr()
                for kc in range(KC):
                    mm(ps_k[:, 0:n], wA[:, kc, 0:128], hT[:, kc, 0:n], start=kc == 0, stop=kc == KC - 1)
                if not isctx:
                    P.dma(rc[:, 0:n], cdr(ropeC_d[:, c0 - NCTX:c0 - NCTX + n]))
                    P.dma(rs[:, 0:n], cdr(ropeS_d[:, c0 - NCTX:c0 - NCTX + n]))
                qk_norm(ps_k, n, kg, None if isctx else c0, T(KT.ap[:, c0:c0 + n], KT_trk[bi]), (qsq, qkn, qt1), psr)
                gla_decay(hT, n, cum, tmp4, psr)
                for pr in range(2):
                    ps_gk = psr()
                    for kc in range(KC):
                        mm(ps_gk[:, 0:n], wA[:, kc, 256 + pr * 128:256 + (pr + 1) * 128], hT[:, kc, 0:n],
                           start=kc == 0, stop=kc == KC - 1)
                    cp(gk_sb[:, pr, 0:n], ps_gk[:, 0:n], e="act")
                for pr in range(2):
                    act(decf[:, pr, 0:n // 128], cum[0][:, pr, 127:n:128], AF.Exp)
                    act(decb[:, pr, c0 // 128:(c0 + n) // 128], cum[1][:, pr, 0:n:128], AF.Exp)
                for i in range(n // 128):
                    gt = c0 // 128 + i
                    cs = slice(i * 128, (i + 1) * 128)
                    ps_v = psr()
                    for kc in range(KC):
                        mm(ps_v[:, 0:128], hT[:, kc, cs], wA[:, kc, 128:256], start=kc == 0, stop=kc == KC - 1)
                    cp(T(VA.ap[:, gt, :, 0:64], VA_trk[bi]),
                       ps_v[:, 0:128].v(ps_v.ap[:, 0:128].rearrange("p (h d) -> p h d", h=2)), e="act")
                    ps_gv = psr()
                    for kc in range(KC):
                        mm(ps_gv[:, :], hT[:, kc, cs], wA[:, kc, 512:1024], start=kc == 0, stop=kc == KC - 1)
                    g_v = gv()
                    cp(g_v, ps_gv, e="act")
                    for d in range(2):
                        for pr in range(2):
                            e_ = Ee()
                            last_col = (i * 128 + 127) if d == 0 else (i * 128)
                            act(e_, cum[d][:, pr, cs], AF.Exp, bias=cum[d][:, pr, last_col:last_col + 1], scale=-1.0)
                            k_ = kstT()
                            tt(k_, gk_sb[:, pr, cs], e_, ALU.mult)
                            tr(PSB[:, (d * 2 + pr) * 128:(d * 2 + pr + 1) * 128], k_, ident_b)
                    cp(kst.v(kst.ap.rearrange("p a b c -> p (a b c)")), PSB[:, 0:512], e="act")
                    ps_u = [psr(), psr()]
                    for d in range(2):
                        for h in range(4):
                            pr, hp = h // 2, h % 2
                            mm(ps_u[d][hp * 64:(hp + 1) * 64, pr * 128:(pr + 1) * 128],
                               kst[:, d, pr, hp * 64:(hp + 1) * 64], g_v[:, h * 128:(h + 1) * 128])
                    s_st = sinst()
                    cp(s_st, S.v(S.ap.rearrange("p a b -> p (a b)")), e="act")
                    P.dma(T(sin_scr[0, gt], trk["sinf"][gt]), s_st)
                    for pr in range(2):
                        stt(S[:, pr, :], S[:, pr, :], decf[:, pr, i:i + 1], ps_u[0][:, pr * 128:(pr + 1) * 128],
                            ALU.mult, ALU.add)
                    u_st = ubst()
                    cp(u_st, ps_u[1][:, 0:256], e="act")
                    P.dma(T(ub_scr[gt], trk["ub"][gt]), u_st)
            mset(S, 0.0)
            order = [1, 0] + list(range(NTILE - 1, 1, -1))
            for gt in (order if phase_on("PA2", l) else []):
                u_st = ubst()
                P.dma(u_st, T(ub_scr[gt], trk["ub"][gt]))
                s_st = sinst()
                cp(s_st, S.v(S.ap.rearrange("p a b -> p (a b)")), e="act")
                P.dma(T(sin_scr[1, gt], trk["sinb"][gt]), s_st)
                for pr in range(2):
                    stt(S[:, pr, :], S[:, pr, :], decb[:, pr, gt:gt + 1], u_st[:, pr * 128:(pr + 1) * 128],
                        ALU.mult, ALU.add)

        P.barrier()
        with contextlib.ExitStack() as st:
            wG = sb(st, "wG", [128, KC, 512], BF16)
            loadw(wG[:, :, 0:256], wview(w_in_d[l], KC, WCOL["gq"][0], 256))
            loadw(wG[:, :, 256:512], wview(w_in_d[l], KC, WCOL["gk"][0], 256))
            wV = sb(st, "wV", [128, KC, 512], BF16)
            loadw(wV, wview(w_in_d[l], KC, WCOL["gv"][0], 512))
            wR = sb(st, "wR", [128, KC, 512], BF16)
            loadw(wR, wview(w_in_d[l], KC, WCOL["gr"][0], 512))
            gng = sb(st, "gng", [128, 512]); P.dma(gng, cdr(gng_d[l]))
            hTs = [sb(st, "hTb%d" % i, [128, KC, 512], BF16) for i in range(2)]
            cum = [sb(st, "cumb%d" % i, [128, 2, 512]) for i in range(2)]
            tmp4 = [sb(st, "g4b_%d" % i, [128, 512]) for i in range(4)]
            E1 = sb(st, "E1", [128, 512])
            qin = [sb(st, "qin%d" % i, [128, 2, 512], BF16) for i in range(2)]
            kin = [sb(st, "kin%d" % i, [128, 2, 512], BF16) for i in range(2)]
            gv = Rot([sb(st, "gvb%d" % i, [128, 512], BF16) for i in range(2)])
            sr = Rot([sb(st, "sr%d" % i, [128, 512]) for i in range(2)])
            attm = [sb(st, "attm%d" % i, [128, 512], BF16) for i in range(2)]
            sinl = [Rot([sb(st, "sinl%d_%d" % (d, i), [128, 2, 128], BF16) for i in range(2)]) for d in range(2)]
            osq = sb(st, "osq", [128, 512]); oss = sb(st, "oss", [128, 4])
            on = sb(st, "on", [128, 512])
            yc = sb(st, "yc", [128, 512], BF16)
            ycT = Rot([sb(st, "ycT%d" % i, [128, 4, 512], BF16) for i in range(2)])
            psr = Rot(PS[0:7])
            for bi, (c0, n, isctx) in enumerate(blocks if phase_on("PB1", l) else []):
                if last and isctx:
                    continue
                hT = hTs[bi % 2]
                P.dma(hT[:, :, 0:n], dr(hT_scr[:, :, c0:c0 + n], "hT", c0, n))
                gla_decay(hT, n, cum, tmp4, psr)
                ps_q = [psr(), psr()]
                ps_kk = [psr(), psr()]
                for pr in range(2):
                    for kc in range(KC):
                        mm(ps_q[pr][:, 0:n], wG[:, kc, pr * 128:(pr + 1) * 128], hT[:, kc, 0:n],
                           start=kc == 0, stop=kc == KC - 1)
                    for kc in range(KC):
                        mm(ps_kk[pr][:, 0:n], wG[:, kc, 256 + pr * 128:256 + (pr + 1) * 128], hT[:, kc, 0:n],
                           start=kc == 0, stop=kc == KC - 1)
                for d in range(2):
                    for pr in range(2):
                        act(E1[:, 0:n], cum[d][:, pr, 0:n], AF.Exp)
                        stt(qin[d][:, pr, 0:n], ps_q[pr][:, 0:n], 0.125, E1[:, 0:n], ALU.mult, ALU.mult)
                        act(E1[:, 0:n], cum[d][:, pr, 0:n], AF.Exp, scale=-1.0)
                        tt(kin[d][:, pr, 0:n], ps_kk[pr][:, 0:n], E1[:, 0:n], ALU.mult)
                yT = ycT()
                for i in range(n // 128 if SUB >= 2 else 0):
                    gt = c0 // 128 + i
                    cs = slice(i * 128, (i + 1) * 128)
                    ps_gv = psr()
                    for kc in range(KC):
                        mm(ps_gv, hT[:, kc, cs], wV[:, kc, :], start=kc == 0, stop=kc == KC - 1)
                    g_v = gv()
                    cp(g_v, ps_gv, e="act")
                    ps_r = psr()
                    for kc in range(KC):
                        mm(ps_r, hT[:, kc, cs], wR[:, kc, :], start=kc == 0, stop=kc == KC - 1)
                    s_r = sr()
                    act(s_r, ps_r, AF.Silu)
                    if SUB < 2.5:
                        continue
                    sl = []
                    for d in range(2):
                        s_ = sinl[d]()
                        P.dma(s_.v(s_.ap.rearrange("p a b -> p (a b)")),
                              T(sin_scr[d, gt], trk["sinf" if d == 0 else "sinb"][gt]))
                        sl.append(s_)
                    for d in range(2):
                        pa = [psr(), psr()]
                        for h in range(4):
                            pr, hp = h // 2, h % 2
                            mm(pa[hp][:, pr * 128:(pr + 1) * 128], kin[d][hp * 64:(hp + 1) * 64, pr, cs],
                               qin[d][hp * 64:(hp + 1) * 64, pr, cs])
                        mk = maskF if d == 0 else maskB
                        for hp in range(2):
                            tt(attm[d].v(attm[d].ap.rearrange("p (pr hp t) -> p pr hp t", pr=2, hp=2)[:, :, hp, :]),
                               pa[hp][:, 0:256].v(pa[hp].ap[:, 0:256].rearrange("p (pr t) -> p pr t", pr=2)),
                               mk[:, 0:256].v(mk.ap[:, 0:256].rearrange("p (pr t) -> p pr t", pr=2)), ALU.mult)
                    po = [psr(), psr()]
                    for h in range(4):
                        pr, hp = h // 2, h % 2
                        oc = slice(h * 128, (h + 1) * 128)
                        od = po[hp][:, pr * 128:(pr + 1) * 128]
                        mm(od, attm[0][:, oc], g_v[:, oc], start=True, stop=False)
                        mm(od, attm[1][:, oc], g_v[:, oc], start=False, stop=False)
                        mm(od, qin[0][hp * 64:(hp + 1) * 64, pr, cs], sl[0][hp * 64:(hp + 1) * 64, pr, :],
                           start=False, stop=False)
                        mm(od, qin[1][hp * 64:(hp + 1) * 64, pr, cs], sl[1][hp * 64:(hp + 1) * 64, pr, :],
                           start=False, stop=True)
                    for hp in range(2):
                        act(osq.v(osq.ap.rearrange("p (pr hp t) -> p pr hp t", pr=2, hp=2)[:, :, hp, :]),
                            po[hp][:, 0:256].v(po[hp].ap[:, 0:256].rearrange("p (pr t) -> p pr t", pr=2)), AF.Square)
                    red(oss, osq.v(osq.ap.rearrange("p (h v) -> p h v", h=4)))
                    act(oss, oss, AF.Sqrt, bias=EPS, scale=1.0 / 128)
                    recip(oss, oss)
                    for h in range(4):
                        pr, hp = h // 2, h % 2
                        oc = slice(h * 128, (h + 1) * 128)
                        stt(on[:, oc], po[hp][:, pr * 128:(pr + 1) * 128], oss[:, h:h + 1], gng[:, oc], ALU.mult, ALU.mult)
                    tt(yc, on, s_r, ALU.mult)
                    if SUB < 7:
                        continue
                    for c in range(4):
                        tr(PSB[:, c * 128:(c + 1) * 128], yc[:, c * 128:(c + 1) * 128], ident_b)
                    cp(yT[:, :, cs], PSB[:, 0:512].v(PSB.ap[:, 0:512].rearrange("p (c t) -> p c t", c=4)), e="act")
                P.dma(dr(yc_scr[:, :, c0:c0 + n], "yc", c0, n), yT[:, :, 0:n])

        P.barrier()
        with contextlib.ExitStack() as st:
            wU = sb(st, "wU", [128, KC, 512], BF16)
            loadw(wU, wview(w_in_d[l], KC, WCOL["mu"][0], 512))
            wMV = sb(st, "wMV", [128, KC, 512], BF16)
            loadw(wMV, wview(w_in_d[l], KC, WCOL["mv"][0], 512))
            wsT = sb(st, "wsT", [128, 4, 128], BF16)
            loadw(wsT, wsT_d[l])
            bsr = sb(st, "bsr", [1, 512], BF16)
            loadw(bsr, bs_d[l])
            mng = sb(st, "mng", [128, 512]); P.dma(mng, cdr(mng_d[l]))
            hTs = [sb(st, "hTc%d" % i, [128, KC, 512], BF16) for i in range(2)]
            gu = sb(st, "gu", [128, 4, 512], BF16)
            gvv = sb(st, "gvv", [128, 512]); gsq = sb(st, "gsq", [128, 512]); gss = sb(st, "gss", [128, 4])
            gx2 = sb(st, "gx2", [128, 512]); gsg = sb(st, "gsg", [128, 512])
            vn = sb(st, "vn", [128, 512], BF16)
            gmT = Rot([sb(st, "gmT%d" % i, [128, 4, 512], BF16) for i in range(2)])
            psr = Rot(PS[0:7])

            def gelu(out, ps_in, n):
                act(gx2[:, 0:n], ps_in, AF.Square)
                ts(gx2[:, 0:n], gx2[:, 0:n], 0.044715, 1.0, ALU.mult, ALU.add)
                tt(gx2[:, 0:n], gx2[:, 0:n], ps_in, ALU.mult)
                act(gsg[:, 0:n], gx2[:, 0:n], AF.Sigmoid, scale=1.5957691216057308)
                tt(out, gsg[:, 0:n], ps_in, ALU.mult)

            for bi, (c0, n, isctx) in enumerate(blocks if phase_on("PB2", l) else []):
                if last and isctx:
                    continue
                hT = hTs[bi % 2]
                P.dma(hT[:, :, 0:n], dr(hT_scr[:, :, c0:c0 + n], "hT", c0, n))
                for g in range(4):
                    ps_u = psr()
                    for kc in range(KC):
                        mm(ps_u[:, 0:n], wU[:, kc, g * 128:(g + 1) * 128], hT[:, kc, 0:n], start=kc == 0, stop=kc == KC - 1)
                    gelu(gu[:, g, 0:n], ps_u[:, 0:n], n)
                gm = gmT()
                for i in range(n // 128):
                    cs = slice(i * 128, (i + 1) * 128)
                    ps_v = psr()
                    for kc in range(KC):
                        mm(ps_v, hT[:, kc, cs], wMV[:, kc, :], start=kc == 0, stop=kc == KC - 1)
                    gelu(gvv, ps_v, 512)
                    tt(gsq, gvv, gvv, ALU.mult)
                    red(gss, gsq.v(gsq.ap.rearrange("p (g c) -> p g c", g=4)))
                    act(gss, gss, AF.Sqrt, bias=EPS, scale=1.0 / 128)
                    recip(gss, gss)
                    for g in range(4):
                        oc = slice(g * 128, (g + 1) * 128)
                        stt(vn[:, oc], gvv[:, oc], gss[:, g:g + 1], mng[:, oc], ALU.mult, ALU.mult)
                    ps_f = psr()
                    for g in range(4):
                        oc = slice(g * 128, (g + 1) * 128)
                        mm(ps_f[:, oc], vn[:, oc], wsT[:, g, :], start=True, stop=False)
                        mm(ps_f[:, oc], ones_b[0:1, 0:128], bsr[0:1, oc], start=False, stop=True)
                    tt(gm[:, :, cs], gu[:, :, cs], ps_f.v(ps_f.ap.rearrange("p (g t) -> p g t", g=4)), ALU.mult)
                P.dma(dr(gm_scr[:, :, c0:c0 + n], "gm", c0, n), gm[:, :, 0:n])

        P.barrier()
        with contextlib.ExitStack() as st:
            wQ = sb(st, "wQ", [128, KC, 512], BF16)
            loadw(wQ, wview(w_in_d[l], KC, WCOL["aq"][0], 512))
            hTs = [sb(st, "hTd%d" % i, [128, KC, 512], BF16) for i in range(2)]
            qT = sb(st, "qT", [128, 4, 512], BF16)
            qsq = sb(st, "qsq3", [128, 512]); qkn = sb(st, "qkn3", [128, 512]); qt1 = sb(st, "qt13", [128, 512])
            rc = sb(st, "rc3", [128, 512]); rs = sb(st, "rs3", [128, 512])
            tmps_rope = (rc, rs)
            pT = Rot([sb(st, "pT%d" % i, [128, 512], BF16) for i in range(3)])
            att = sb(st, "att", [128, 4, 512], BF16)
            rden = Rot([sb(st, "rden%d" % i, [128, 1]) for i in range(4)])
            atT = Rot([sb(st, "atT%d" % i, [128, 4, 512], BF16) for i in range(2)])
            nm8 = sb(st, "nm8", [128, 1]); mset(nm8, -8.0)
            acc = PS[0:4]
            pss = Rot(PS[4:6])
            psq = Rot(PS[4:7])
            for bi, (c0, n, isctx) in enumerate(blocks if phase_on("PB3", l) else []):
                if last and isctx:
                    continue
                hT = hTs[bi % 2]
                P.dma(hT[:, :, 0:n], dr(hT_scr[:, :, c0:c0 + n], "hT", c0, n))
                if not isctx:
                    P.dma(rc[:, 0:n], cdr(ropeC_d[:, c0 - NCTX:c0 - NCTX + n]))
                    P.dma(rs[:, 0:n], cdr(ropeS_d[:, c0 - NCTX:c0 - NCTX + n]))
                for c in range(4):
                    ps_q = psq()
                    for kc in range(KC):
                        mm(ps_q[:, 0:n], wQ[:, kc, c * 128:(c + 1) * 128], hT[:, kc, 0:n], start=kc == 0, stop=kc == KC - 1)
                    qk_norm(ps_q, n, qg, None if isctx else c0, qT[:, c, 0:n], (qsq, qkn, qt1), psq)
                nkt = 2 if isctx else NTILE
                nq = n // 128
                for h in range(8):
                    c, hp = h % 4, h // 4
                    prt = slice(hp * 64, (hp + 1) * 64)
                    for kt in range(nkt):
                        kb = 0 if kt < 2 else 1 + (kt - 2) // 4
                        ps_s = pss()
                        mm(ps_s[:, 0:n], T(KT.ap[prt, kt * 128:(kt + 1) * 128], KT_trk[kb]), qT[prt, c, 0:n])
                        p_ = pT()
                        act(p_[:, 0:n], ps_s[:, 0:n], AF.Exp, bias=nm8, scale=0.125)
                        for qt in range(nq):
                            mm(acc[qt][:, 0:65], p_[:, qt * 128:(qt + 1) * 128], T(VA.ap[:, kt, hp, :], VA_trk[kb]),
                               start=kt == 0, stop=kt == nkt - 1)
                    for qt in range(nq):
                        r_ = rden()
                        recip(r_, acc[qt][:, 64:65])
                        ts(att[:, qt, h * 64:(h + 1) * 64], acc[qt][:, 0:64], r_, None, ALU.mult)
                aT = atT()
                for qt in range(nq):
                    for c in range(4):
                        tr(PSB[:, c * 128:(c + 1) * 128], att[:, qt, c * 128:(c + 1) * 128], ident_b)
                    cp(aT[:, :, qt * 128:(qt + 1) * 128],
                       PSB[:, 0:512].v(PSB.ap[:, 0:512].rearrange("p (c t) -> p c t", c=4)), e="act")
                P.dma(dr(at_scr[:, :, c0:c0 + n], "at", c0, n), aT[:, :, 0:n])
        P.barrier()
        MS.close()

        P.barrier()
        with contextlib.ExitStack() as st:
            wgt = [sb(st, "wgt%d" % i, [128, KC, 1024], BF16) for i in range(3)]
            for i, nm in enumerate(("gA", "gB", "gC")):
                for hh in range(2):
                    loadw(wgt[i][:, :, hh * 512:(hh + 1) * 512], wview(w_in_d[l], KC, WCOL[nm][0] + hh * 512, 512))
            wbr = [sb(st, "wbr%d" % i, [128, 4, 1024], BF16) for i in range(3)]
            for i in range(3):
                loadw(wbr[i], wview(wbr_d[i][l], 4, 0, 1024))
            wo = sb(st, "wo", [128, KC, 1024], BF16)
            for hh in range(2):
                loadw(wo[:, :, hh * 512:(hh + 1) * 512], wview(wout_d[l], KC, hh * 512, 512))
            hTs = [sb(st, "hTe%d" % i, [128, KC, 512], BF16) for i in range(2)]
            srcs = [[sb(st, "src%d_%d" % (k, i), [128, 4, 512], BF16) for i in range(2)] for k in range(3)]
            xts = [sb(st, "xte%d" % i, [128, KC, 512]) for i in range(1)]
            sg = Rot([sb(st, "sg%d" % i, [128, 512]) for i in range(3)])
            tb = [sb(st, "tb%d" % i, [128, 512]) for i in range(3)]
            mrg = sb(st, "mrg", [128, KC, 512], BF16)
            xmo = Rot([sb(st, "xmo%d" % i, [128, KC, 512]) for i in range(1)])
            psr = Rot(PS[0:7])
            scr_list = ((gm_scr, "gm"), (at_scr, "at"), (yc_scr, "yc"))
            for bi, (c0, n, isctx) in enumerate(blocks if phase_on("PB4", l) else []):
                if last and isctx:
                    continue
                s = 1 if isctx else 0
                hT = hTs[bi % 2]
                P.dma(hT[:, :, 0:n], dr(hT_scr[:, :, c0:c0 + n], "hT", c0, n))
                sr_ = []
                for k in range(3):
                    t_ = srcs[k][bi % 2]
                    P.dma(t_[:, :, 0:n], dr(scr_list[k][0][:, :, c0:c0 + n], scr_list[k][1], c0, n))
                    sr_.append(t_)
                x_t = xts[0]
                P.dma(x_t[:, :, 0:n], dr(xsrc[:, :, c0:c0 + n], xsrc_nm, c0, n))
                for m in range(KC):
                    mc = slice(m * 128, (m + 1) * 128)
                    for k in range(3):
                        ps_g = psr()
                        for kc in range(KC):
                            mm(ps_g[:, 0:n], wgt[k][:, kc, mc], hT[:, kc, 0:n], start=kc == 0, stop=kc == KC - 1)
                        s_g = sg()
                        act(s_g[:, 0:n], ps_g[:, 0:n], AF.Sigmoid)
                        ps_y = psr()
                        for kc in range(4):
                            mm(ps_y[:, 0:n], wbr[k][:, kc, mc], sr_[k][:, kc, 0:n], start=kc == 0, stop=kc == 3)
                        tt(tb[k][:, 0:n], ps_y[:, 0:n], s_g[:, 0:n], ALU.mult)
                    tt(tb[0][:, 0:n], tb[0][:, 0:n], tb[1][:, 0:n], ALU.add)
                    tt(mrg[:, m, 0:n], tb[0][:, 0:n], tb[2][:, 0:n], ALU.add)
                xo = xmo()
                for m in range(KC):
                    mc = slice(m * 128, (m + 1) * 128)
                    ps_o = psr()
                    for kc in range(KC):
                        mm(ps_o[:, 0:n], wo[:, kc, mc], mrg[:, kc, 0:n], start=kc == 0, stop=kc == KC - 1)
                    stt(xo[:, m, 0:n], ps_o[:, 0:n], gate1(s)[:, m:m + 1], x_t[:, m, 0:n], ALU.mult, ALU.add)
                P.dma(dr(xm_scr[:, :, c0:c0 + n], "xm", c0, n), xo[:, :, 0:n])

        P.barrier()
        with contextlib.ExitStack() as st:
            wup = sb(st, "wup", [128, KC, 2 * FH], BF16)
            for pc in range(11):
                loadw(wup[:, :, pc * 512:(pc + 1) * 512], wview(wup_d[l], KC, pc * 512, 512))
            wdn = sb(st, "wdn", [128, NJ, 1024], BF16)
            for hh in range(4):
                loadw(wdn[:, :, hh * 256:(hh + 1) * 256], wview(wdn_d[l], NJ, hh * 256, 256))
            cw = sb(st, "cw", [128, 2 * NJ, 3]); P.dma(cw, cdr(cw_d[l]))
            cb = sb(st, "cb", [128, 2 * NJ]); P.dma(cb, cdr(cb_d[l]))
            NW = 258
            xms = [sb(st, "xms%d" % i, [128, KC, NW]) for i in range(1)]
            sq = sb(st, "sqc", [128, KC, NW])
            ssum = sb(st, "ssumc", [128, NW])
            rstd = sb(st, "rstdc", [128, NW])
            tmpn = Rot([sb(st, "tmpnc%d" % i, [128, NW]) for i in range(1)])
            h2 = sb(st, "h2", [128, KC, NW], BF16)
            a_gs = Rot([sb(st, "a_g%d" % i, [128, NW]) for i in range(2)])
            a_vs = Rot([sb(st, "a_v%d" % i, [128, NW]) for i in range(2)])
            cvgs = Rot([sb(st, "cvg%d" % i, [128, 256]) for i in range(2)])
            cvvs = Rot([sb(st, "cvv%d" % i, [128, 256]) for i in range(1)])
            actT = sb(st, "actT", [128, NJ, 256], BF16)
            xo2 = Rot([sb(st, "xo2_%d" % i, [128, KC, 256]) for i in range(1)])
            fo = Rot([sb(st, "fo%d" % i, [128, KC, 256]) for i in range(1)])
            psr = Rot(PS[0:6])
            psn = PS[6]
            for bi, (c0, n, isctx) in enumerate(cblocks if phase_on("PC", l) else []):
                if last and isctx:
                    continue
                s = 1 if isctx else 0
                lo_end = c0 == 0 or c0 == NCTX
                hi_end = c0 + n == NCTX or c0 + n == NT
                xm = xms[0]
                a0 = c0 - 1 if not lo_end else c0
                a1 = c0 + n + 1 if not hi_end else c0 + n
                o0 = a0 - (c0 - 1)
                if lo_end:
                    mset(xm[:, :, 0:1], 0.0)
                if hi_end:
                    mset(xm[:, :, n + 1:n + 2], 0.0)
                P.dma(xm[:, :, o0:o0 + (a1 - a0)], dr(xm_scr[:, :, a0:a1], "xm", a0, a1 - a0))
                norm_mod((sq, ssum, rstd, tmpn, psn), xm, NW, gs2[:, s, :], shift2(s), h2)
                for j in range(NJ):
                    cvg, cvv = cvgs(), cvvs()
                    for half, a_sb, cv in ((0, a_gs(), cvg), (1, a_vs(), cvv)):
                        jj = half * NJ + j
                        ps = psr()
                        for kc in range(KC):
                            mm(ps[:, 0:NW], wup[:, kc, jj * 128:(jj + 1) * 128], h2[:, kc, :], start=kc == 0, stop=kc == KC - 1)
                        cp(a_sb, ps[:, 0:NW], e="act")
                        if lo_end:
                            mset(a_sb[:, 0:1], 0.0)
                        if hi_end:
                            mset(a_sb[:, n + 1:n + 2], 0.0)
                        ts(cv, a_sb[:, 0:n], cw[:, jj, 0:1], cb[:, jj:jj + 1], ALU.mult, ALU.add)
                        stt(cv, a_sb[:, 1:n + 1], cw[:, jj, 1:2], cv, ALU.mult, ALU.add)
                        stt(cv, a_sb[:, 2:n + 2], cw[:, jj, 2:3], cv, ALU.mult, ALU.add)
                    act(cvg, cvg, AF.Silu)
                    tt(actT[:, j, :], cvg, cvv, ALU.mult)
                xo = xo2()
                for m in range(KC):
                    mc = slice(m * 128, (m + 1) * 128)
                    ps_o = psr()
                    for j in range(NJ):
                        mm(ps_o[:, 0:n], wdn[:, j, mc], actT[:, j, :], start=j == 0, stop=j == NJ - 1)
                    stt(xo[:, m, :], ps_o[:, 0:n], gate2(s)[:, m:m + 1], xm[:, m, 1:n + 1], ALU.mult, ALU.add)
                if not last:
                    P.dma(dr(x1_scr[:, :, c0:c0 + n], "x1", c0, n), xo)
                else:
                    f_ = fo()
                    norm_mod((sq, ssum, rstd, tmpn, psn), xo, n, fng, zero8, f_)
                    P.dma(cdr(outT[:, :, c0 - NCTX:c0 - NCTX + n]), f_)
        P.barrier()
        LS.close()

    P.finish("sp")
    ES.close()
    P.close()
    return nc


def _fm(v):
    kc = v.shape[-1] // 128
    return np.ascontiguousarray(np.swapaxes(v.reshape(v.shape[:-1] + (kc, 128)), -1, -2))


def _consts(SEQ):
    cst = np.zeros((128, 5, 512), np.float32)
    cst[:, 0, 0:128] = 1.0
    for hb in range(2):
        cst[hb * 64:(hb + 1) * 64, 0, 128 + hb * 64:128 + (hb + 1) * 64] = 1.0
    for m in range(128):
        d = m % 64
        base = m - d
        blk, r = d // 32, d % 32
        if r < 16:
            cst[base + blk * 32 + r + 16, 0, 256 + m] = -1.0
        else:
            cst[base + blk * 32 + r - 16, 0, 256 + m] = 1.0
    cst[:, 0, 384:512] = np.eye(128, dtype=np.float32)
    cst[:, 1, :] = 1.0
    cst[:, 1, 0::128] = 0.0
    j = np.arange(128)[:, None]
    i = np.arange(128)[None, :]
    cst[:, 2, :] = np.tile((j <= i).astype(np.float32), (1, 4))
    cst[:, 3, :] = np.tile((j >= i).astype(np.float32), (1, 4))
    t = np.arange(SEQ)
    inv = (10000.0 ** (-np.arange(16, dtype=np.float32) / 16)).astype(np.float32)
    ang_r = (t // 64).astype(np.float32)[None, :] * inv[:, None]
    ang_c = (t % 64).astype(np.float32)[None, :] * inv[:, None]
    C = np.zeros((128, SEQ), np.float32)
    S = np.zeros((128, SEQ), np.float32)
    for m in range(128):
        d = m % 64
        ang = ang_r if d < 32 else ang_c
        C[m] = np.cos(ang[d % 16]).astype(np.float32)
        S[m] = np.sin(ang[d % 16]).astype(np.float32)
    return cst, C, S


def _prep(inp, SEQ, DEPTH):
    f = np.float32
    g = {k: np.asarray(v) for k, v in inp.items()}
    IN = {"mu": (0, 512), "mv": (512, 512), "aq": (1024, 512), "ak": (1536, 128), "av": (1664, 128),
          "gq": (1792, 256), "gk": (2048, 256), "gv": (2304, 512), "af": (2816, 16), "ab": (2832, 16),
          "gr": (2848, 512), "gA": (3360, 1024), "gB": (4384, 1024), "gC": (5408, 1024)}
    w_in = g["w_in"]
    win = np.zeros((DEPTH, D, WIN_W), f)
    for nm, (o, w) in WCOL.items():
        if nm == "dec":
            win[:, :, o:o + 16] = w_in[:, :, IN["af"][0]:IN["af"][0] + 16]
            win[:, :, o + 32:o + 48] = w_in[:, :, IN["ab"][0]:IN["ab"][0] + 16]
        elif nm == "aq":
            src = w_in[:, :, IN["aq"][0]:IN["aq"][0] + 512].reshape(DEPTH, D, 8, 64)
            order = [0, 4, 1, 5, 2, 6, 3, 7]
            win[:, :, o:o + 512] = src[:, :, order, :].reshape(DEPTH, D, 512)
        else:
            win[:, :, o:o + w] = w_in[:, :, IN[nm][0]:IN[nm][0] + w]
    cst, C, S = _consts(SEQ)
    shared = {
        "w_ada": g["w_ada"].astype(f), "b_ada": _fm(g["b_ada"].reshape(DEPTH, 48 * 128)).reshape(DEPTH, 128, 48),
        "n1g": _fm(g["norm1_g"]), "n2g": _fm(g["norm2_g"]), "fng": _fm(g["final_norm_g"]),
        "w_in": win,
        "qg": np.tile(g["q_norm_g"], (1, 2)).reshape(DEPTH, 128, 1).astype(f),
        "kg": np.tile(g["k_norm_g"], (1, 2)).reshape(DEPTH, 128, 1).astype(f),
        "mng": np.ascontiguousarray(np.broadcast_to(g["gmlp_norm_g"][:, None, :], (DEPTH, 128, 512))).astype(f),
        "wsT": np.ascontiguousarray(np.transpose(g["w_spatial"], (0, 3, 1, 2))).astype(f),
        "bs": g["b_spatial"].reshape(DEPTH, 1, 512).astype(f),
        "gng": np.ascontiguousarray(np.broadcast_to(g["gla_norm_g"][:, None, :], (DEPTH, 128, 512))).astype(f),
        "wbr0": g["w_br_a"], "wbr1": g["w_br_b"], "wbr2": g["w_br_c"],
        "wout": g["w_out"], "wup": g["w_ffn_up"], "wdn": g["w_ffn_down"],
        "ropeC": C, "ropeS": S, "cst": cst,
    }
    w2 = np.zeros((DEPTH, 48, 256), f)
    w2[:, 0:16] = g["w_alpha2"][:, 0]
    w2[:, 32:48] = g["w_alpha2"][:, 1]
    shared["w2"] = w2
    ba = g["b_alpha"].reshape(DEPTH, 2, 2, 128)
    shared["ba"] = np.ascontiguousarray(np.transpose(ba, (0, 3, 1, 2))).reshape(DEPTH, 128, 4).astype(f)
    cwv = g["conv_w"].reshape(DEPTH, 3, 2 * NJ, 128)
    shared["cw"] = np.ascontiguousarray(np.transpose(cwv, (0, 3, 2, 1))).astype(f)
    shared["cb"] = np.ascontiguousarray(np.transpose(g["conv_b"].reshape(DEPTH, 2 * NJ, 128), (0, 2, 1))).astype(f)
    shared = {k: np.ascontiguousarray(v, dtype=f) for k, v in shared.items()}
    shared["b_ada"] = np.ascontiguousarray(
        np.transpose(g["b_ada"].reshape(DEPTH, 48, 128), (0, 2, 1))).astype(f)
    B = g["x"].shape[0]
    per = []
    for b in range(B):
        X = np.concatenate([g["ctx"][b], g["x"][b]], 0)
        xin = np.ascontiguousarray(np.transpose(X.reshape(-1, KC, 128), (2, 1, 0))).astype(f)
        cc = np.stack([g["c"][b], g["c_ctx"]], -1)
        cT = np.ascontiguousarray(np.transpose(cc.reshape(KC, 128, 2), (1, 0, 2))).astype(f)
        per.append({"xin": xin, "cT": cT})
    return shared, per


_NC_CACHE = {}


def run(inp, dbg=False, n_cores=8):
    x = np.asarray(inp["x"])
    B, SEQ, _ = x.shape
    DEPTH = np.asarray(inp["w_in"]).shape[0]
    key = (SEQ, DEPTH, dbg)
    if key not in _NC_CACHE:
        _NC_CACHE[key] = build(SEQ, DEPTH, dbg)
    nc = _NC_CACHE[key]
    shared, per = _prep(inp, SEQ, DEPTH)
    in_maps = []
    for c in range(n_cores):
        m = dict(shared)
        m.update(per[c % B])
        in_maps.append(m)
    res = run_bass_kernel_spmd(nc, in_maps, core_ids=list(range(n_cores)))
    out = np.empty((B, SEQ, D), np.float32)
    for b in range(B):
        o = res.results[b]["outT"]
        out[b] = np.transpose(o, (2, 1, 0)).reshape(SEQ, D)
    return out, res


def kernel(**inputs):
    out, _ = run(inputs)
    return out
```
